# Optimizing a Trainium2 kernel written in Bass

```python
import jax, jax.numpy as jnp
from jax import lax
import numpy as np

D_MODEL = 1024
BATCH = 4
SEQ = 8192
DEPTH = 2

N_MIXERS = 2
HEAD_DIM = 64
N_SLOTS = 8
DILATED_GROUPS = ((128, 1), (512, 4), (2048, 16))
N_GROUPS = 3
GROUP_WIDTH = N_SLOTS * HEAD_DIM
ROT_DIM = HEAD_DIM // 4
ROPE_THETA = 500000.0
CONV_KERNEL = 31
CONV_INNER = D_MODEL
FFN_DIM = 2816
FFN_CONV = 3
EPS = 1e-6

kernel_name = "hybrid_dilated_attn_conformer_convffn"


def _rmsnorm(x, g):
    xf = x.astype(jnp.float32)
    y = xf * lax.rsqrt(jnp.mean(xf * xf, axis=-1, keepdims=True) + EPS)
    return (y * g.astype(jnp.float32)).astype(x.dtype)


def _layernorm(x, g, b):
    xf = x.astype(jnp.float32)
    mu = jnp.mean(xf, axis=-1, keepdims=True)
    var = jnp.mean(jnp.square(xf - mu), axis=-1, keepdims=True)
    y = (xf - mu) * lax.rsqrt(var + EPS)
    return (y * g.astype(jnp.float32) + b.astype(jnp.float32)).astype(x.dtype)


def _causal_depthwise_conv(x, w):
    k, c = w.shape
    return lax.conv_general_dilated(
        x, w[:, None, :].astype(x.dtype), window_strides=(1,),
        padding=[(k - 1, 0)], dimension_numbers=("NWC", "WIO", "NWC"),
        feature_group_count=c)


def _partial_rope(t, positions):
    half = ROT_DIM // 2
    inv_freq = ROPE_THETA ** (-jnp.arange(half, dtype=jnp.float32) / half)
    ang = positions.astype(jnp.float32)[:, :, None] * inv_freq
    cos = jnp.cos(ang)[:, :, None, None, :]
    sin = jnp.sin(ang)[:, :, None, None, :]
    tr = t[..., :ROT_DIM].astype(jnp.float32)
    t1, t2 = tr[..., :half], tr[..., half:]
    rot = jnp.concatenate([t1 * cos - t2 * sin, t2 * cos + t1 * sin], axis=-1)
    return jnp.concatenate([rot.astype(t.dtype), t[..., ROT_DIM:]], axis=-1)


def _banded_causal_attention(q, k, v, span):
    n, h, l, dh = q.shape
    nb = l // span
    qb = q.reshape(n, h, nb, span, dh)
    kb = k.reshape(n, h, nb, span, dh)
    vb = v.reshape(n, h, nb, span, dh)

    def with_prev(t):
        prev = jnp.pad(t, ((0, 0), (0, 0), (1, 0), (0, 0), (0, 0)))[:, :, :-1]
        return jnp.concatenate([prev, t], axis=3)

    kc, vc = with_prev(kb), with_prev(vb)
    s = jnp.einsum("nhbqd,nhbkd->nhbqk", qb, kc,
                   preferred_element_type=jnp.float32) * (HEAD_DIM ** -0.5)
    qi = jnp.arange(span)[:, None] + span
    ki = jnp.arange(2 * span)[None, :]
    dist = qi - ki
    band = (dist >= 0) & (dist <= span)
    has_prev = (jnp.arange(nb)[:, None, None] > 0) | (ki[None] >= span)
    mask = band[None] & has_prev
    s = jnp.where(mask, s, -jnp.inf)
    m = jnp.max(s, axis=-1, keepdims=True)
    p = jnp.exp(s - m)
    denom = jnp.sum(p, axis=-1)
    o = jnp.einsum("nhbqk,nhbkd->nhbqd", p, vc.astype(jnp.float32)) / denom[..., None]
    lse = m[..., 0] + jnp.log(denom)
    return o.reshape(n, h, l, dh), lse.reshape(n, h, l)


def _dilated_group_attention(q, k, v, window, dilation):
    b, s, h, dh = q.shape
    span = window // dilation
    l = s // dilation
    lp = -(-l // span) * span

    def to_phases(t):
        t = t.reshape(b, l, dilation, h, dh).transpose(0, 2, 3, 1, 4)
        t = t.reshape(b * dilation, h, l, dh)
        return jnp.pad(t, ((0, 0), (0, 0), (0, lp - l), (0, 0)))

    o, lse = _banded_causal_attention(to_phases(q), to_phases(k), to_phases(v), span)
    o = o[:, :, :l].reshape(b, dilation, h, l, dh).transpose(0, 3, 1, 2, 4)
    lse = lse[:, :, :l].reshape(b, dilation, h, l).transpose(0, 3, 1, 2)
    return o.reshape(b, s, h, dh), lse.reshape(b, s, h)


def _dilated_attention(h, positions, w_qkv, w_o):
    b, s, _ = h.shape
    qkv = (h @ w_qkv).reshape(b, s, 3, N_GROUPS, N_SLOTS, HEAD_DIM)
    q = _partial_rope(qkv[:, :, 0], positions)
    k = _partial_rope(qkv[:, :, 1], positions)
    v = qkv[:, :, 2]
    outs, lses = [], []
    for g, (window, dilation) in enumerate(DILATED_GROUPS):
        o_g, lse_g = _dilated_group_attention(q[:, :, g], k[:, :, g], v[:, :, g],
                                              window, dilation)
        outs.append(o_g)
        lses.append(lse_g)
    o = jnp.stack(outs, axis=0)
    wgt = jax.nn.softmax(jnp.stack(lses, axis=0), axis=0)
    mixed = jnp.sum(wgt[..., None] * o, axis=0).astype(h.dtype)
    return mixed.reshape(b, s, GROUP_WIDTH) @ w_o


def _conformer_conv(h, w_pw1, b_pw1, w_dw, b_dw, ln_g, ln_b, w_pw2, b_pw2):
    a, gate = jnp.split(h @ w_pw1 + b_pw1, 2, axis=-1)
    u = a * jax.nn.sigmoid(gate)
    u = _causal_depthwise_conv(u, w_dw) + b_dw
    u = jax.nn.silu(_layernorm(u, ln_g, ln_b))
    return u @ w_pw2 + b_pw2


def _conv_ffn(h, w_up, w_dw, b_dw, w_down):
    ug = _causal_depthwise_conv(h @ w_up, w_dw) + b_dw
    up, gate = jnp.split(ug, 2, axis=-1)
    return (jax.nn.silu(gate) * up) @ w_down


def setup_inputs(seed: int = 0) -> dict:
    key = jax.random.key(seed)
    ks = jax.random.split(key, 20)
    n_attn = (DEPTH + N_MIXERS - 1) // N_MIXERS
    n_conv = DEPTH // N_MIXERS
    f32 = jnp.float32
    nrm = lambda k, shape, fan: jax.random.normal(k, shape, f32) * (fan ** -0.5)
    small = lambda k, shape: 0.02 * jax.random.normal(k, shape, f32)
    x = jax.random.normal(ks[0], (BATCH, SEQ, D_MODEL), f32)
    offset = jax.random.randint(ks[1], (BATCH, 1), 0, 4096, dtype=jnp.int32)
    positions = (jnp.arange(SEQ, dtype=jnp.int32)[None, :] + offset).astype(jnp.int32)
    return {
        "x": x,
        "positions": positions,
        "norm_g": 1.0 + small(ks[2], (DEPTH, 4, D_MODEL)),
        "attn_w_qkv": nrm(ks[3], (n_attn, D_MODEL, 3 * N_GROUPS * GROUP_WIDTH), D_MODEL),
        "attn_w_o": nrm(ks[4], (n_attn, GROUP_WIDTH, D_MODEL), GROUP_WIDTH),
        "conv_w_pw1": nrm(ks[5], (n_conv, D_MODEL, 2 * CONV_INNER), D_MODEL),
        "conv_b_pw1": small(ks[6], (n_conv, 2 * CONV_INNER)),
        "conv_w_dw": nrm(ks[7], (n_conv, CONV_KERNEL, CONV_INNER), CONV_KERNEL),
        "conv_b_dw": small(ks[8], (n_conv, CONV_INNER)),
        "conv_ln_g": 1.0 + small(ks[9], (n_conv, CONV_INNER)),
        "conv_ln_b": small(ks[10], (n_conv, CONV_INNER)),
        "conv_w_pw2": nrm(ks[11], (n_conv, CONV_INNER, D_MODEL), CONV_INNER),
        "conv_b_pw2": small(ks[12], (n_conv, D_MODEL)),
        "ffn_w_up": nrm(ks[13], (DEPTH, D_MODEL, 2 * FFN_DIM), D_MODEL),
        "ffn_w_dw": nrm(ks[14], (DEPTH, FFN_CONV, 2 * FFN_DIM), FFN_CONV),
        "ffn_b_dw": small(ks[15], (DEPTH, 2 * FFN_DIM)),
        "ffn_w_down": nrm(ks[16], (DEPTH, FFN_DIM, D_MODEL), FFN_DIM),
    }


def reference(x, positions, norm_g, attn_w_qkv, attn_w_o, conv_w_pw1, conv_b_pw1,
              conv_w_dw, conv_b_dw, conv_ln_g, conv_ln_b, conv_w_pw2, conv_b_pw2,
              ffn_w_up, ffn_w_dw, ffn_b_dw, ffn_w_down):
    for i in range(DEPTH):
        g = norm_g[i]
        j = i // N_MIXERS
        hn = _rmsnorm(x, g[0])
        if i % N_MIXERS == 0:
            y = _dilated_attention(hn, positions, attn_w_qkv[j], attn_w_o[j])
        else:
            y = _conformer_conv(hn, conv_w_pw1[j], conv_b_pw1[j], conv_w_dw[j],
                                conv_b_dw[j], conv_ln_g[j], conv_ln_b[j],
                                conv_w_pw2[j], conv_b_pw2[j])
        x = x + _rmsnorm(y, g[1])
        hn = _rmsnorm(x, g[2])
        y = _conv_ffn(hn, ffn_w_up[i], ffn_w_dw[i], ffn_b_dw[i], ffn_w_down[i])
        x = x + _rmsnorm(y, g[3])
    return x
```

```python
import numpy as np
from contextlib import ExitStack
import concourse.bass as bass
import concourse.mybir as mybir
from concourse.bass_utils import run_bass_kernel_spmd

F32 = mybir.dt.float32
BF16 = mybir.dt.bfloat16
I32 = mybir.dt.int32
AF = mybir.ActivationFunctionType
ALU = mybir.AluOpType

D = 1024
SEQ = 8192
NB = 4
NCORES = 8
HALO = 2560
NLOC = HALO + 4096
NBLK = NLOC // 128
NTILE = NLOC // 512
HALO_BLK = 19
FFN = 2816
EPS = 1e-6
DIL = {1: 1, 2: 4, 3: 16}
NSLOT = 3

PP_GT = 0
PP_BPW1 = PP_GT + 32
PP_WDW = PP_BPW1 + 16
PP_BDW = PP_WDW + 248
PP_LNG = PP_BDW + 8
PP_LNB = PP_LNG + 8
PP_FW = PP_LNB + 8
PP_FB = PP_FW + 264
PP_N = PP_FB + 88
C_ID = 0
C_MASK = 128
C_INVF = C_MASK + 256
C_N = C_INVF + 8


class Prog:
    ENG = ["pe", "act", "dve", "pool", "sp"]

    def __init__(self):
        self.ins = {e: [] for e in self.ENG}
        self.lastw = {}
        self.rd = {}
        self.dma_count = {}

    def emit(self, eng, fn, reads=(), writes=(), dma=None):
        deps = set()
        for r in reads:
            if r in self.lastw:
                deps.add(self.lastw[r])
            if isinstance(r, tuple) and r[0] == "ps":
                for k, v in self.rd.get(r, {}).items():
                    if k[1] != eng:
                        deps.add(k + (v,))
        for w in writes:
            if w in self.lastw:
                deps.add(self.lastw[w])
            for k, v in self.rd.get(w, {}).items():
                deps.add(k + (v,))
        idx = len(self.ins[eng])
        if dma is None:
            ev = ("e", eng, idx)
        else:
            self.dma_count[dma] = self.dma_count.get(dma, 0) + 16
            ev = ("d", dma, self.dma_count[dma])
        waits = [d for d in deps if not (d[0] == "e" and d[1] == eng and eng in ("pe", "sp"))]
        self.ins[eng].append(dict(fn=fn, waits=waits, dma=dma))
        for r in reads:
            m = self.rd.setdefault(r, {})
            k = ev[:2]
            m[k] = max(m.get(k, -1), ev[2])
        for w in writes:
            self.lastw[w] = ev
            self.rd[w] = {}
        return ev

    def barrier(self, dkeys=None):
        last = {}
        for e in self.ENG:
            for i in range(len(self.ins[e]) - 1, -1, -1):
                if self.ins[e][i]["fn"] is not None and self.ins[e][i]["dma"] is None:
                    last[e] = i
                    break
        dm = {k: v for k, v in self.dma_count.items() if dkeys is None or k in dkeys}
        for e in self.ENG:
            waits = [("e", o, i) for o, i in last.items() if o != e] + [("d", k, v) for k, v in dm.items()]
            self.ins[e].append(dict(fn=None, waits=waits, dma=None))

    def replay(self, nc, block, esem, dsem):
        sig = {e: set() for e in self.ENG}
        for e in self.ENG:
            for it in self.ins[e]:
                for w in it["waits"]:
                    if w[0] == "e":
                        sig[w[1]].add(w[2])
        rank = {e: {i: r + 1 for r, i in enumerate(sorted(sig[e]))} for e in self.ENG}

        def run(eng, eo):
            known = {}
            for i, it in enumerate(self.ins[eng]):
                need = {}
                for w in it["waits"]:
                    if w[0] == "e":
                        key, val = ("e", w[1]), rank[w[1]][w[2]]
                    else:
                        key, val = ("d", w[1]), w[2]
                    if known.get(key, 0) < val:
                        need[key] = max(need.get(key, 0), val)
                for key, val in need.items():
                    eo.wait_ge(esem[key[1]] if key[0] == "e" else dsem[key[1]], val)
                    known[key] = val
                if it["fn"] is None:
                    continue
                bi = it["fn"](eo)
                if it["dma"] is not None:
                    bi.then_inc(dsem[it["dma"]], 16)
                elif i in rank[eng]:
                    bi.then_inc(esem[eng], 1)

        @block.tensor
        def _(e):
            run("pe", e)

        @block.scalar
        def _(e):
            run("act", e)

        @block.vector
        def _(e):
            run("dve", e)

        @block.gpsimd
        def _(e):
            run("pool", e)

        @block.sync
        def _(e):
            run("sp", e)


def chunk_plan():
    ids = {}
    n = 0
    for nm in ["v3", "v2", "v1", "k3", "k2", "k1", "q3", "q2", "q1", "wo"]:
        ids[nm] = n
        n += 1
    for L in range(2):
        for i in range(11):
            ids[("up", L, i)] = n
            n += 1
        for hf in range(2):
            for pc in range(3):
                ids[("dn", L, hf, pc)] = n
                n += 1
    for i in range(4):
        ids[("pw1", i)] = n
        n += 1
    for hf in range(2):
        ids[("pw2", hf)] = n
        n += 1
    return ids, n


def s1_kv_groups(t):
    return [3] if t <= 2 else ([3, 2] if t == 3 else [3, 2, 1])


def s1_tile_chunks(t):
    kvg = s1_kv_groups(t)
    ch = ["v%d" % g for g in kvg] + ["k%d" % g for g in kvg]
    if t >= 4:
        ch += ["q3", "q2", "q1", "wo"]
    return ch


def s2_tile_chunks():
    seq = []
    seq += [("up", 0, i) for i in range(11)]
    seq += [("dn", 0, hf, pc) for hf in range(2) for pc in range(3)]
    seq += [("pw1", i) for i in range(4)]
    seq += [("pw2", hf) for hf in range(2)]
    seq += [("up", 1, i) for i in range(11)]
    seq += [("dn", 1, hf, pc) for hf in range(2) for pc in range(3)]
    return seq


def build_program(debug=False, phase=2):
    nc = bass.Bass("TRN2", target_bir_lowering=False)
    P = Prog()

    def din(name, shape, dt=F32):
        return nc.dram_tensor(name, shape, dt, kind="ExternalInput").ap()

    xs_d = din("xs", [NLOC, D])
    pos_d = din("pos", [128, NBLK], I32)
    tm_d = din("tmask", [128, NBLK])
    flag_d = din("flag", [128, 1])
    wqkv_d = din("w_qkv", [D, 4608])
    wo_d = din("w_o", [512, D])
    wpw1_d = din("w_pw1", [D, 2048])
    wpw2_d = din("w_pw2", [D, D])
    wup_d = din("w_up", [2, D, 2 * FFN])
    wdn_d = din("w_down", [2, FFN, D])
    pp_d = din("pp", [128, PP_N])
    bc_d = din("bc", [5, D])
    cst_d = din("cst", [128, C_N])
    out_d = nc.dram_tensor("out", [4096, D], F32, kind="ExternalOutput").ap()
    x1_kind = "ExternalOutput" if debug else "Internal"
    x1_d = nc.dram_tensor("x1s", [33 * 128, D], F32, kind=x1_kind).ap()
    cids, NCH = chunk_plan()
    W_d = nc.dram_tensor("wbf", [NCH, 128, 4096], BF16, kind="Internal").ap()
    Vd = {g: nc.dram_tensor(f"vd{g}", [NLOC, 520], BF16, kind="Internal").ap() for g in (2, 3)}

    es0 = ExitStack()
    E0 = es0.enter_context

    def SB(stack, name, shape, dt):
        return stack.enter_context(nc.sbuf_tensor("sb_" + name, shape, dt))

    identF = SB(es0, "identF", [128, 128], F32)
    identB = SB(es0, "identB", [128, 128], BF16)
    maskUL = SB(es0, "maskUL", [128, 2, 128], BF16)
    onesdiv = SB(es0, "onesdiv", [128, 128], BF16)
    invf = SB(es0, "invf", [128, 8], F32)
    pp = SB(es0, "pp", [128, PP_N], F32)
    tm = SB(es0, "tm", [128, NBLK], F32)
    flag = SB(es0, "flag", [128, 1], F32)
    wsl = SB(es0, "wsl", [128, NSLOT, 4096], BF16)
    ps = [E0(nc.psum_tensor(f"ps{k}", [128, 512], F32)) for k in range(8)]
    psb = [p[:].bitcast(BF16) for p in ps]

    def prekey(nm):
        if nm in ("v3", "k3"):
            return "pre0"
        if nm in ("v2", "v1", "k2", "k1"):
            return "pre1"
        if nm in ("q3", "q2", "q1", "wo"):
            return "pre2"
        if isinstance(nm, tuple) and nm[0] in ("up", "dn") and nm[1] == 0:
            return "pre3"
        return "pre4"

    import os as _os
    _nopre = bool(_os.environ.get("NOPRE"))

    _pre_list = []

    def pre_dma(nm, out_ap, in_ap):
        _pre_list.append((nm, out_ap, in_ap))

    def prepass_some(n):
        if not _pre_list:
            return
        pw = [("d", k, v) for k, v in P.dma_count.items() if str(k).startswith("pre")]
        if pw:
            P.ins["pool"].append(dict(fn=None, waits=pw, dma=None))
        for _ in range(min(n, len(_pre_list))):
            nm, o, i = _pre_list.pop(0)
            P.emit("pool", lambda e, o=o, i=i: e.dma_start(out=o, in_=i), reads=(), writes=[("wd", cids[nm])], dma=prekey(nm))

    def kcview(ap2d):
        return ap2d.rearrange("(kc p) n -> p kc n", p=128)

    cg_col = {"q1": 0, "q2": 1, "q3": 2, "k1": 3, "k2": 4, "k3": 5, "v1": 6, "v2": 7, "v3": 8}
    order = ["v3", "k3", "v2", "v1", "k2", "k1", "q3", "q2", "q1", "wo"]
    for nm in order:
        ch = W_d[cids[nm]]
        if nm == "wo":
            pre_dma(nm, ch.rearrange("p (kc n) -> p kc n", kc=4), kcview(wo_d))
        else:
            c0 = cg_col[nm] * 512
            pre_dma(nm, ch.rearrange("p (kc n) -> p kc n", kc=8), kcview(wqkv_d[:, c0:c0 + 512]))
    for nm in s2_tile_chunks():
        ch = W_d[cids[nm]]
        if nm[0] == "up":
            _, L, i = nm
            v = ch.rearrange("p (kc u n) -> p kc u n", kc=8, u=2)
            pre_dma(nm, v[:, :, 0, :], kcview(wup_d[L][:, 256 * i:256 * i + 256]))
            pre_dma(nm, v[:, :, 1, :], kcview(wup_d[L][:, FFN + 256 * i:FFN + 256 * i + 256]))
        elif nm[0] == "dn":
            _, L, hf, pc = nm
            k0, k1 = [(0, 8), (8, 16), (16, 22)][pc]
            v = ch.rearrange("p (kc n) -> p kc n", kc=8)
            pre_dma(nm, v[:, 0:k1 - k0, :], kcview(wdn_d[L][k0 * 128:k1 * 128, hf * 512:hf * 512 + 512]))
        elif nm[0] == "pw1":
            i = nm[1]
            v = ch.rearrange("p (kc u n) -> p kc u n", kc=8, u=2)
            pre_dma(nm, v[:, :, 0, :], kcview(wpw1_d[:, 256 * i:256 * i + 256]))
            pre_dma(nm, v[:, :, 1, :], kcview(wpw1_d[:, 1024 + 256 * i:1024 + 256 * i + 256]))
        else:
            hf = nm[1]
            pre_dma(nm, ch.rearrange("p (kc n) -> p kc n", kc=8), kcview(wpw2_d[:, hf * 512:hf * 512 + 512]))
    def prepass_fix(keys):
        for nm, cid in cids.items():
            k = prekey(nm)
            if k in keys:
                P.lastw[("wd", cid)] = ("d", k, P.dma_count[k])

    prepass_some(5)
    prepass_some(5)
    prepass_fix(("pre0", "pre1", "pre2"))

    wseq = []
    for t in range(NTILE):
        wseq += s1_tile_chunks(t)
    for _ in range(9):
        wseq += s2_tile_chunks()
    wstate = dict(issued=0, used=0)

    def wnext(expect):
        n = wstate["used"]
        assert wseq[n] == expect, (n, wseq[n], expect)
        while wstate["issued"] < min(len(wseq), n + NSLOT):
            m = wstate["issued"]
            s = m % NSLOT
            nv = 3072 if (isinstance(wseq[m], tuple) and wseq[m][0] == "dn" and wseq[m][3] == 2) else 4096
            P.emit("sp", lambda e, o=wsl[:, s, 0:nv], i=W_d[cids[wseq[m]]][:, 0:nv]: e.dma_start(out=o, in_=i),
                   reads=[("wd", cids[wseq[m]])], writes=[("w", s)], dma=("w", s))
            wstate["issued"] += 1
        wstate["used"] += 1
        s = n % NSLOT
        return wsl[:, s, :], ("w", s)

    def ACT(out, in_, func, reads, writes, **kw):
        return P.emit("act", lambda e: e.activation(out=out, in_=in_, func=func, **kw), reads, writes)

    def TT(eng, out, in0, in1, op, reads, writes):
        return P.emit(eng, lambda e: e.tensor_tensor(out=out, in0=in0, in1=in1, op=op), reads, writes)

    def TS(eng, out, in0, s1, s2, op0, op1, reads, writes):
        if s2 is None:
            return P.emit(eng, lambda e: e.tensor_scalar(out=out, in0=in0, scalar1=s1, scalar2=None, op0=op0), reads, writes)
        return P.emit(eng, lambda e: e.tensor_scalar(out=out, in0=in0, scalar1=s1, scalar2=s2, op0=op0, op1=op1), reads, writes)

    def STT(out, in0, scalar, in1, op0, op1, reads, writes):
        return P.emit("dve", lambda e: e.scalar_tensor_tensor(out=out, in0=in0, scalar=scalar, in1=in1, op0=op0, op1=op1),
                      reads, writes)

    def CP(eng, out, in_, reads, writes):
        if eng == "act":
            return P.emit("act", lambda e: e.activation(out=out, in_=in_, func=AF.Copy), reads, writes)
        return P.emit(eng, lambda e: e.tensor_copy(out=out, in_=in_), reads, writes)

    def MM(out, lhsT, rhs, start, stop, reads, writes):
        return P.emit("pe", lambda e: e.matmul(out, lhsT=lhsT, rhs=rhs, start=start, stop=stop), reads, writes)

    def TR(out, in_, ident, reads, writes):
        return P.emit("pe", lambda e: e.transpose(out, in_, ident), reads, writes)

    def DMA(eng, out, in_, reads, writes, key):
        return P.emit(eng, lambda e: e.dma_start(out=out, in_=in_), reads, writes, dma=key)

    bank_state = dict(n=0, ring=[0, 1, 2, 3])

    def mmbank():
        r = bank_state["ring"]
        k = r[bank_state["n"] % len(r)]
        bank_state["n"] += 1
        return k

    cosT = SB(es0, "cosT", [128, NBLK, 8], F32)
    sinT = SB(es0, "sinT", [128, NBLK, 8], F32)
    esi = ExitStack()
    cstf = SB(esi, "cstf", [128, C_N], F32)
    DMA("sp", cstf[:], cst_d, [], ["cstf"], "c0_1")
    DMA("sp", pp[:], pp_d, [], ["pp"], "c0_2")
    DMA("sp", tm[:], tm_d, [], ["tm"], "c0_3")
    DMA("sp", flag[:], flag_d, [], ["flag"], "c0_4")
    CP("dve", identF[:], cstf[:, C_ID:C_ID + 128], ["cstf"], ["identF"])
    CP("dve", identB[:], cstf[:, C_ID:C_ID + 128], ["cstf"], ["identB"])
    CP("dve", maskUL[:], cstf[:, C_MASK:C_MASK + 256].rearrange("p (u n) -> p u n", u=2), ["cstf"], ["maskUL"])
    CP("dve", invf[:], cstf[:, C_INVF:C_INVF + 8], ["cstf"], ["invf"])
    P.emit("dve", lambda e: e.memset(onesdiv[:], 1.0 / 1024.0), [], ["onesdiv"])

    if True:
        esr = esi
        posi = SB(esr, "posi", [128, NBLK], I32)
        posf = SB(esr, "posf", [128, NBLK], F32)
        ang = SB(esr, "ang", [128, NBLK, 8], F32)
        ang2 = SB(esr, "ang2", [128, NBLK, 8], F32)
        kf = SB(esr, "kf", [128, NBLK, 8], F32)
        ki = SB(esr, "ki", [128, NBLK, 8], I32)
        DMA("sp", posi[:], pos_d, [], ["posi"], "c0_5")
        CP("dve", posf[:], posi[:], ["posi"], ["posf"])
        TT("dve", ang[:], posf[:].unsqueeze(2).to_broadcast([128, NBLK, 8]),
           invf[:].unsqueeze(1).to_broadcast([128, NBLK, 8]), ALU.mult, ["posf", "invf"], ["ang"])
        C1 = 6.28125
        C2 = float(2.0 * np.pi - 6.28125)
        for (dst, shift) in ((sinT, 0.0), (cosT, float(np.pi / 2))):
            src = ang
            if shift != 0.0:
                TS("dve", ang2[:], ang[:], shift, None, ALU.add, None, ["ang"], ["ang2"])
                src = ang2
            sname = "ang" if src is ang else "ang2"
            TS("dve", kf[:], src[:], float(1.0 / (2.0 * np.pi)), None, ALU.mult, None, [sname], ["kf"])
            CP("dve", ki[:], kf[:], ["kf"], ["ki"])
            CP("dve", kf[:], ki[:], ["ki"], ["kf"])
            STT(dst[:], kf[:], -C1, src[:], ALU.mult, ALU.add, ["kf", sname], [("rt", id(dst))])
            STT(dst[:], kf[:], -C2, dst[:], ALU.mult, ALU.add, ["kf"], [("rt", id(dst))])
            TS("dve", dst[:], dst[:], float(np.pi), float(-np.pi), ALU.min, ALU.max, [], [("rt", id(dst))])
            ACT(dst[:], dst[:], AF.Sin, [("rt", id(dst))], [("rt", id(dst))])
    ROPE_RES = [("rt", id(sinT)), ("rt", id(cosT))]
    P.barrier(dkeys=[k for k in P.dma_count if str(k).startswith("c0")])
    esi.close()
    if phase == 0:
        P.barrier()
        with ExitStack() as esm:
            esem = {e: esm.enter_context(nc.semaphore("sem_" + e)) for e in Prog.ENG}
            dsem = {}
            for i, k in enumerate(sorted(P.dma_count.keys(), key=str)):
                dsem[k] = esm.enter_context(nc.semaphore(f"dsem{i}"))
            block = esm.enter_context(nc.Block())
            P.replay(nc, block, esem, dsem)
        es0.close()
        return nc

    def rstd_from_ss(ss_ap, rs_ap, reads, res):
        ACT(rs_ap, ss_ap, AF.Sqrt, reads, [res], scale=1.0 / 1024.0, bias=EPS)
        P.emit("dve", lambda e: e.reciprocal(out=rs_ap, in_=rs_ap), [res], [res])

    def n_stats(x_ap, x_res, junk, st, slot):
        sres = ("nst", id(st), slot)
        ACT(junk, x_ap, AF.Square, [x_res], [("junk", id(junk)), sres], accum_out=st[:, slot, 0:1])
        ACT(st[:, slot, 1:2], st[:, slot, 0:1], AF.Sqrt, [sres], [sres], scale=1.0 / 1024.0, bias=EPS)

    def n_scale_tr(x_ap, x_res, hn_tm, st, slot):
        sres = ("nst", id(st), slot)
        rs = st[:, slot, 1:2]
        P.emit("dve", lambda e: e.reciprocal(out=rs, in_=rs), [sres], [sres])
        hres = ("hn_tm", id(hn_tm), slot)
        ACT(hn_tm[:, slot, :], x_ap, AF.Identity, [x_res, sres], [hres], scale=rs)
        k = mmbank()
        for kc in range(8):
            TR(psb[k][:, kc * 128:(kc + 1) * 128], hn_tm[:, slot, kc * 128:(kc + 1) * 128], identB[:],
               [hres, "identB"], [("ps", k)])
        return k

    def n_evac(k, gcol, hnT_dst, hnT_res, extra_scale=None):
        g_b = pp[:, gcol:gcol + 8].unsqueeze(2).to_broadcast([128, 8, 128])
        TT("dve", hnT_dst, psb[k].rearrange("p (kc n) -> p kc n", kc=8), g_b, ALU.mult, [("ps", k), "pp"], [hnT_res])
        if extra_scale is not None:
            TS("dve", hnT_dst, hnT_dst, extra_scale, None, ALU.mult, None, ["flag"], [hnT_res])

    def norm_to_T(x_ap, x_res, gcol, hn_tm, junk, st, slot, hnT_dst, hnT_res, extra_scale=None):
        n_stats(x_ap, x_res, junk, st, slot)
        k = n_scale_tr(x_ap, x_res, hn_tm, st, slot)
        n_evac(k, gcol, hnT_dst, hnT_res, extra_scale)

    def pn_stats(yh, yres, st, slot, junk):
        sres = ("pst", id(st), slot)
        for hf in range(2):
            ACT(junk[:, 0:512], yh[hf], AF.Square, [yres[hf]], [("junk", id(junk)), sres], accum_out=st[:, slot, hf:hf + 1])
        TT("dve", st[:, slot, 2:3], st[:, slot, 0:1], st[:, slot, 1:2], ALU.add, [sres], [sres])
        rstd_from_ss(st[:, slot, 2:3], st[:, slot, 3:4], [sres], sres)

    def pn_apply(yh, yres, gbc_ap, gres, xr_ap, xr_res, st, slot, tmp):
        sres = ("pst", id(st), slot)
        for hf in range(2):
            tr = ("pn_tmp", id(tmp))
            STT(tmp[:, 0:512], yh[hf], st[:, slot, 3:4], gbc_ap[:, hf * 512:(hf + 1) * 512], ALU.mult, ALU.mult,
                [yres[hf], sres, gres], [tr])
            TT("dve", xr_ap[:, hf * 512:(hf + 1) * 512], xr_ap[:, hf * 512:(hf + 1) * 512], tmp[:, 0:512], ALU.add,
               [tr, xr_res], [xr_res])

    def postnorm_resid(yh, yres, gbc_ap, gres, xr_ap, xr_res, st, slot, junk, tmp):
        pn_stats(yh, yres, st, slot, junk)
        pn_apply(yh, yres, gbc_ap, gres, xr_ap, xr_res, st, slot, tmp)

    def postnorm_then_norm(nb, ybanks, gbc_ap, gres, xr, xr_res_f, next_gcol, next_extra):
        for b in range(nb):
            k0_, k1_ = ybanks[b]
            pn_stats([ps[k0_][:], ps[k1_][:]], [("ps", k0_), ("ps", k1_)], pst2, b, junk2)
        trk = {}

        def n_a(b):
            n_stats(xr[:, b, :], xr_res_f(b), junk2[:], nst2, b)

        def n_b(b):
            trk[b] = n_scale_tr(xr[:, b, :], xr_res_f(b), hn2, nst2, b)

        def n_c(b):
            n_evac(trk[b], next_gcol, hnT2[:, :, b * 128:(b + 1) * 128], ("hnT2", b), next_extra)

        for b in range(nb + 2):
            if b < nb:
                k0_, k1_ = ybanks[b]
                pn_apply([ps[k0_][:], ps[k1_][:]], [("ps", k0_), ("ps", k1_)], gbc_ap, gres, xr[:, b, :], xr_res_f(b), pst2, b, pntmp2)
                n_a(b)
            if 1 <= b <= nb:
                n_b(b - 1)
            if 2 <= b:
                n_c(b - 2)

    es1 = ExitStack()
    xsl = SB(es1, "xsl", [128, 2, D], F32)
    junk1 = SB(es1, "junk1", [128, D], BF16)
    hn1 = SB(es1, "hn1", [128, 2, D], BF16)
    nst1 = SB(es1, "nst1", [128, 2, 4], F32)
    hnT = SB(es1, "hnT", [128, 8, 512], BF16)
    stg = SB(es1, "stg", [128, 2, 4, 512], BF16)
    rtmp = SB(es1, "rtmp", [128, 4, 256], F32)
    qT = SB(es1, "qT", [128, 12, 512], BF16)
    kT = {1: SB(es1, "k1T", [128, 4, 640], BF16), 2: SB(es1, "k2T", [128, 4, 1024], BF16),
          3: SB(es1, "k3T", [128, 4, 4096], BF16)}
    V1c = SB(es1, "V1c", [128, 5, 8, 65], BF16)
    V2c = SB(es1, "V2c", [128, 2, 4, 520], BF16)
    V3c = SB(es1, "V3c", [128, 2, 16, 520], BF16)
    Vst = SB(es1, "Vst", [128, 4, 8, 65], BF16)
    Pt = SB(es1, "Pt", [128, 2, 1024], BF16)
    accT = SB(es1, "accT", [128, 4, 512], F32)
    mixed = SB(es1, "mixed", [128, 4, 512], BF16)
    mixedT = SB(es1, "mixedT", [128, 4, 512], BF16)
    x1stg = SB(es1, "x1stg", [128, D], F32)
    pst1 = SB(es1, "pst1", [128, 2, 4], F32)
    pntmp1 = SB(es1, "pntmp1", [128, 512], F32)
    rden = SB(es1, "rden", [128, 2, 4], F32)
    gbc1 = SB(es1, "gbc1", [128, D], F32)
    if _os.environ.get("KDEBUG"):
        print("sweep1 sbuf bytes remaining:", nc.sbuf_bytes_remaining)
    DMA("sp", gbc1[:], bc_d[0].partition_broadcast(128), [], ["gbc1"], "c0_6")
    for t_ in (kT[1], kT[2], kT[3], V1c, V2c, V3c, Vst):
        P.emit("pool", lambda e, t_=t_: e.memset(t_[:], 0.0), [], [("cache", id(t_))])
    CACHE_INIT = [("cache", id(t_)) for t_ in (kT[1], kT[2], kT[3], V1c, V2c, V3c, Vst)]
    P.emit("pool", lambda e: e.memset(accT[:], 0.0), [], ["accT"])

    bank_state.update(n=0, ring=[0, 1, 2, 3])
    sb_state = dict(s=0, o=0, p=0, x=0, q=0)

    def sbank():
        k = 4 + sb_state["s"] % 2
        sb_state["s"] += 1
        return k

    def obank():
        k = 6 + sb_state["o"] % 2
        sb_state["o"] += 1
        return k

    def kbase(g, t, b):
        if g == 1:
            return ((4 * t + b) % 5) * 128
        if g == 2:
            return (t % 2) * 512 + b * 128
        return ((t // 4) % 2) * 2048 + (t % 4) * 512 + b * 128

    stage1_done = set()

    def stage1(t):
        stage1_done.add(t)
        for b in range(4):
            B = 4 * t + b
            xslot = sb_state["x"] % 2
            sb_state["x"] += 1
            DMA("sp", xsl[:, xslot, :], xs_d[B * 128:(B + 1) * 128, :], [], [("xsl", xslot)], ("xsl", xslot))
            norm_to_T(xsl[:, xslot, :], ("xsl", xslot), PP_GT + 0, hn1, junk1[:], nst1, xslot,
                      hnT[:, :, b * 128:(b + 1) * 128], ("hnT", b))

    _s1_tiles = int(_os.environ.get("S1_TILES", NTILE))
    _s1_stage = int(_os.environ.get("S1_STAGE", 9))
    for t in range(NTILE):
        if t >= _s1_tiles:
            break
        chunks = s1_tile_chunks(t)
        prepass_some(6)
        if t == 11:
            while _pre_list:
                prepass_some(6)
            prepass_fix(("pre3", "pre4"))
        kv_groups = s1_kv_groups(t)
        qblocks = [] if t <= 3 else ([3] if t == 4 else [0, 1, 2, 3])
        if t not in stage1_done:
            stage1(t)
        if _s1_stage < 2:
            continue
        bank_state["ring"] = list(range(8))
        for nm in chunks:
            if nm == "wo":
                continue
            Wap, Wres = wnext(nm)
            W3 = Wap.rearrange("p (kc n) -> p kc n", kc=8)
            kind, g = nm[0], int(nm[1])
            if kind == "v" and _os.environ.get("S1_NOV"):
                continue
            if kind != "v" and _os.environ.get("S1_NOK"):
                continue
            blks = list(qblocks if kind == "q" else range(4))
            if kind == "v":
                for b in blks:
                    B = 4 * t + b
                    k = mmbank()
                    for kc in range(8):
                        MM(ps[k][:], hnT[:, kc, b * 128:(b + 1) * 128], W3[:, kc, :], kc == 0, kc == 7,
                           [("hnT", b), Wres], [("ps", k)])
                    pv = ps[k][:].rearrange("p (h d) -> p h d", d=64)
                    if g == 1:
                        dst = V1c[:, B % 5, :, :]
                        dres = ("v1", B % 5)
                    else:
                        dst = Vst[:, b, :, :]
                        dres = ("vst", b)
                    ACT(dst[:, :, 0:64], pv, AF.Copy, [("ps", k)] + CACHE_INIT, [dres])
                    CP("pool", dst[:, :, 64:65], tm[:, B:B + 1].unsqueeze(1).to_broadcast([128, 8, 1]), ["tm"], [dres])
            else:
                sl = sb_state["q"] % 2
                sb_state["q"] += 1
                b0, nbk = blks[0], len(blks)
                for b in blks:
                    k = mmbank()
                    for kc in range(8):
                        MM(ps[k][:], hnT[:, kc, b * 128:(b + 1) * 128], W3[:, kc, :], kc == 0, kc == 7,
                           [("hnT", b), Wres], [("ps", k)])
                    ACT(stg[:, sl, b, :], ps[k][:], AF.Copy, [("ps", k)], [("stg", sl, b)])
                sv = stg[:, sl, b0:b0 + nbk, :].rearrange("p b (h d) -> p b h d", d=64)
                B0 = 4 * t + b0
                cb = cosT[:, B0:B0 + nbk, :].unsqueeze(2).to_broadcast([128, nbk, 8, 8])
                sn = sinT[:, B0:B0 + nbk, :].unsqueeze(2).to_broadcast([128, nbk, 8, 8])
                tmp = [rtmp[:, i, 0:nbk * 64].rearrange("p (b h d) -> p b h d", b=nbk, h=8) for i in range(4)]
                t1, t2 = sv[:, :, :, 0:8], sv[:, :, :, 8:16]
                sres = [("stg", sl, b) for b in blks]
                TT("dve", tmp[0], t1, cb, ALU.mult, sres + ROPE_RES, ["rt0"])
                TT("dve", tmp[1], t2, sn, ALU.mult, sres + ROPE_RES, ["rt1"])
                TT("dve", tmp[2], t2, cb, ALU.mult, sres + ROPE_RES, ["rt2"])
                TT("dve", tmp[3], t1, sn, ALU.mult, sres + ROPE_RES, ["rt3"])
                TT("dve", t1, tmp[0], tmp[1], ALU.subtract, ["rt0", "rt1"], sres)
                TT("dve", t2, tmp[2], tmp[3], ALU.add, ["rt2", "rt3"], sres)
                for b in blks:
                    k2 = mmbank()
                    for c in range(4):
                        TR(psb[k2][:, c * 128:(c + 1) * 128], stg[:, sl, b, c * 128:(c + 1) * 128], identB[:],
                           [("stg", sl, b), "identB"], [("ps", k2)])
                    src = psb[k2][:, 0:512].rearrange("p (c n) -> p c n", c=4)
                    if kind == "q":
                        CP("act", qT[:, (g - 1) * 4:g * 4, b * 128:(b + 1) * 128], src, [("ps", k2)], [("qT", g, b)])
                    else:
                        o = kbase(g, t, b)
                        kres = ("k", g, (4 * t + b) % 5) if g == 1 else (("k", g, t % 2) if g == 2 else ("k", g, (t // 4) % 2))
                        CP("act", kT[g][:, :, o:o + 128], src, [("ps", k2)] + CACHE_INIT, [kres])
            if kind == "v" and g in (2, 3) and not _os.environ.get("S1_NOVDMA"):
                rows = Vd[g][t * 512:(t + 1) * 512, :]
                DMA("sp", rows.rearrange("(b p) f -> p b f", p=128), Vst[:].rearrange("p b h d -> p b (h d)"),
                    [("vst", b) for b in range(4)], [("vd", g)], ("vd", g))
                if g == 2:
                    DMA("sp", V2c[:, t % 2, :, :], rows.rearrange("(a r) f -> a r f", r=4),
                        [("vd", g)] + CACHE_INIT, [("v2", t % 2)], ("v2c", t % 2))
                else:
                    j = t % 4
                    DMA("sp", V3c[32 * j:32 * j + 32, (t // 4) % 2, :, :], rows.rearrange("(a r) f -> a r f", r=16),
                        [("vd", g)] + CACHE_INIT, [("v3", (t // 4) % 2)], ("v3c", (t // 4) % 2))
        if not qblocks or _s1_stage < 3:
            if t + 1 < min(NTILE, _s1_tiles):
                stage1(t + 1)
            continue
        bank_state["ring"] = [0, 1]
        Wo_ap, Wo_res = wnext("wo")
        Wo3 = Wo_ap.rearrange("p (kc n) -> p kc n", kc=4)
        q_lo = qblocks[0] * 128
        units = []
        for b in qblocks:
            B = 4 * t + b
            units.append(dict(g=1, a0=0, nq=128, qs=[b * 128], qd=1, nph=1,
                              kp=[((B - 1) % 5) * 128], ko=[(B % 5) * 128], kd=1,
                              vp=lambda h, f, B=B: V1c[:, (B - 1) % 5, h, :], vo=lambda h, f, B=B: V1c[:, B % 5, h, :],
                              rk=[("k", 1, (B - 1) % 5), ("k", 1, B % 5)], rv=[("v1", (B - 1) % 5), ("v1", B % 5)],
                              rq=[("qT", 1, b)]))
        for r in range(4):
            a0 = q_lo // 4
            nq = (512 - q_lo) // 4
            units.append(dict(g=2, a0=a0, nq=nq, qs=[q_lo + r], qd=4, nph=1,
                              kp=[((t - 1) % 2) * 512 + r], ko=[(t % 2) * 512 + r], kd=4,
                              vp=lambda h, f, r=r: V2c[:, (t - 1) % 2, r, h * 65:(h + 1) * 65],
                              vo=lambda h, f, r=r: V2c[:, t % 2, r, h * 65:(h + 1) * 65],
                              rk=[("k", 2, 0), ("k", 2, 1)], rv=[("v2", 0), ("v2", 1)],
                              rq=[("qT", 2, b) for b in qblocks]))
        for r0 in range(0, 16, 4):
            sbp = (t // 4) % 2
            a0 = (t % 4) * 32 + q_lo // 16
            nq = (512 - q_lo) // 16
            units.append(dict(g=3, a0=a0, nq=nq, qd=16, kd=16, nph=4, r0=r0,
                              qs=[q_lo + r0 + f for f in range(4)],
                              kp=[(1 - sbp) * 2048 + r0 + f for f in range(4)], ko=[sbp * 2048 + r0 + f for f in range(4)],
                              vp=lambda h, f, r0=r0, sbp=sbp: V3c[:, 1 - sbp, r0 + f, h * 65:(h + 1) * 65],
                              vo=lambda h, f, r0=r0, sbp=sbp: V3c[:, sbp, r0 + f, h * 65:(h + 1) * 65],
                              rk=[("k", 3, 0), ("k", 3, 1)], rv=[("v3", 0), ("v3", 1)],
                              rq=[("qT", 3, b) for b in qblocks]))
        if _os.environ.get("S1_G"):
            units = [u for u in units if str(u["g"]) in _os.environ["S1_G"]]
        _att = int(_os.environ.get("S1_ATT", 9))

        def stage_a(hq, u):
            g, nq, a0, nph = u["g"], u["nq"], u["a0"], u["nph"]
            sp_ = sb_state["s"] % 2
            sb_state["s"] += 1
            sbk = [4 + 2 * sp_, 5 + 2 * sp_]
            ncol = 4 * nph * nq
            Sv = [ps[sbk[eo]][:, 0:ncol].rearrange("p (i f u n) -> p i f u n", i=2, f=nph, u=2) for eo in range(2)]
            for hh in range(4):
                h = hq * 4 + hh
                c = (g - 1) * 4 + h // 2
                pr = slice(64 * (h % 2), 64 * (h % 2) + 64)
                for f in range(nph):
                    qs = u["qs"][f]
                    qap = qT[pr, c, qs:qs + u["qd"] * (nq - 1) + 1:u["qd"]]
                    for ui2, off in enumerate((u["kp"][f], u["ko"][f])):
                        kap = kT[g][pr, h // 2, off:off + u["kd"] * 127 + 1:u["kd"]]
                        MM(Sv[h % 2][:, hh // 2, f, ui2, :], kap, qap, True, True, u["rk"] + u["rq"], [("ps", sbk[h % 2])])
            psl = sb_state["p"] % 2
            sb_state["p"] += 1
            for eo in range(2):
                ACT(Pt[:, psl, eo * ncol:(eo + 1) * ncol], ps[sbk[eo]][:, 0:ncol], AF.Exp, [("ps", sbk[eo])], [("Pt", psl, eo)], scale=0.125)
            Pm = Pt[:, psl, 0:2 * ncol].rearrange("p (j u n) -> p j u n", j=4 * nph, u=2)
            mk = maskUL[:, :, a0:a0 + nq].unsqueeze(1).to_broadcast([128, 4 * nph, 2, nq])
            pres = [("Pt", psl, 0), ("Pt", psl, 1)]
            P.emit("dve", lambda e, Pm=Pm, mk=mk: e.tensor_tensor(out=Pm, in0=Pm, in1=mk, op=ALU.mult), pres + ["maskUL"], pres)
            Pv = Pt[:, psl, 0:2 * ncol].rearrange("p (e i f u n) -> p e i f u n", e=2, i=2, f=nph, u=2)
            return Pv, pres

        def stage_b(hq, u, Pv, pres):
            g, nq, nph = u["g"], u["nq"], u["nph"]
            ob = 2 + sb_state["o"] % 2
            sb_state["o"] += 1
            Ov = ps[ob][:, 0:4 * nph * nq].rearrange("p (h f n) -> p h f n", h=4, f=nph)
            for hh in range(4):
                h = hq * 4 + hh
                for f in range(nph):
                    MM(Ov[0:65, hh, f, :], u["vp"](h, f), Pv[:, h % 2, hh // 2, f, 0, :], True, False, pres + u["rv"], [("ps", ob)])
                    MM(Ov[0:65, hh, f, :], u["vo"](h, f), Pv[:, h % 2, hh // 2, f, 1, :], False, True, pres + u["rv"], [("ps", ob)])
            if nph == 1:
                qs = u["qs"][0]
                dst = accT[0:65, :, qs:qs + u["qd"] * (nq - 1) + 1:u["qd"]]
                src = Ov[0:65, :, 0, :]
            else:
                dst = accT[0:65, :, q_lo:q_lo + 16 * nq].rearrange("p h (a r) -> p h r a", r=16)[:, :, u["r0"]:u["r0"] + nph, :]
                src = Ov[0:65, :, :, :]
            if g == 1:
                CP("act", dst, src, [("ps", ob)], ["accT"])
            else:
                TT("dve", dst, dst, src, ALU.add, [("ps", ob)], ["accT"])

        def normalise(hq):
            for b in (qblocks if _s1_stage >= 4 else []):
                k = mmbank()
                Tv = ps[k][:, 0:260].rearrange("p (h d) -> p h d", h=4)
                for hh in range(4):
                    TR(Tv[:, hh, :], accT[0:65, hh, b * 128:(b + 1) * 128], identF[0:65, 0:65], ["accT", "identF"], [("ps", k)])
                rs_ = ("rden", b % 2)
                TS("dve", rden[:, b % 2, :], Tv[:, :, 64], 1e-30, None, ALU.max, None, [("ps", k)], [rs_])
                P.emit("dve", lambda e, b=b: e.reciprocal(out=rden[:, b % 2, :], in_=rden[:, b % 2, :]), [rs_], [rs_])
                mres = ("mixed", b)
                mv = mixed[:, b, hq * 256:(hq + 1) * 256].rearrange("p (h d) -> p h d", h=4)
                TT("dve", mv, Tv[:, :, 0:64], rden[:, b % 2, :].unsqueeze(2).to_broadcast([128, 4, 64]), ALU.mult,
                   [("ps", k), rs_], [mres])
                if hq == 1:
                    k2 = mmbank()
                    for c in range(4):
                        TR(psb[k2][:, c * 128:(c + 1) * 128], mixed[:, b, c * 128:(c + 1) * 128], identB[:],
                           [mres, "identB"], [("ps", k2)])
                    CP("act", mixedT[:, :, b * 128:(b + 1) * 128], psb[k2][:, 0:512].rearrange("p (c n) -> p c n", c=4),
                       [("ps", k2)], [("mixedT", b)])

        for hq in range(2):
            pend = None
            for u in units + [None]:
                cur = None
                if u is not None:
                    cur = (u,) + stage_a(hq, u)
                if pend is not None:
                    stage_b(hq, pend[0], pend[1], pend[2])
                pend = cur
            if hq == 0 and t + 1 < min(NTILE, _s1_tiles):
                stage1(t + 1)
            if hq == 1:
                bank_state["ring"] = [0, 1, 4, 5, 6, 7]
            normalise(hq)
        for b in (qblocks if _s1_stage >= 5 else []):
            B = 4 * t + b
            yb = []
            for hf in range(2):
                k = mmbank()
                for kc in range(4):
                    MM(ps[k][:], mixedT[:, kc, b * 128:(b + 1) * 128], Wo3[:, kc, hf * 512:(hf + 1) * 512], kc == 0, kc == 3,
                       [("mixedT", b), Wo_res], [("ps", k)])
                yb.append(k)
            DMA("sp", x1stg[:], xs_d[B * 128:(B + 1) * 128, :], [], ["x1stg"], "x1in")
            postnorm_resid([ps[yb[0]][:], ps[yb[1]][:]], [("ps", yb[0]), ("ps", yb[1])], gbc1, "gbc1",
                           x1stg, "x1stg", pst1, b % 2, junk1, pntmp1)
            DMA("sp", x1_d[(B - HALO_BLK) * 128:(B - HALO_BLK + 1) * 128, :], x1stg[:], ["x1stg"], [("x1d", B)], "x1out")

    while _pre_list:
        prepass_some(6)
    prepass_fix(("pre3", "pre4"))
    P.barrier()
    es1.close()
    if phase == 1:
        if "x1out" in P.dma_count:
            P.ins["sp"].append(dict(fn=None, waits=[("d", "x1out", P.dma_count["x1out"])], dma=None))
        with ExitStack() as esm:
            esem = {e: esm.enter_context(nc.semaphore("sem_" + e)) for e in Prog.ENG}
            dsem = {}
            for i, k in enumerate(sorted(P.dma_count.keys(), key=str)):
                dsem[k] = esm.enter_context(nc.semaphore(f"dsem{i}"))
            block = esm.enter_context(nc.Block())
            P.replay(nc, block, esem, dsem)
        es0.close()
        return nc

    es2 = ExitStack()
    XR = SB(es2, "XR", [128, 2, 4, D], F32)
    junk2 = SB(es2, "junk2", [128, D], BF16)
    hn2 = SB(es2, "hn2", [128, 4, D], BF16)
    nst2 = SB(es2, "nst2", [128, 4, 4], F32)
    hnT2 = SB(es2, "hnT2", [128, 8, 512], BF16)
    hT = SB(es2, "hT", [128, 22, 512], BF16)
    zsb = SB(es2, "zsb", [128, 2, 514], BF16)
    dgf = SB(es2, "dgf", [128, 2, 3, 128], BF16)
    cacc = SB(es2, "cacc", [128, 2, 512], F32)
    corr = SB(es2, "corr", [128, 2, 2, 22, 2], F32)
    dg31 = SB(es2, "dg31", [128, 2, 31, 128], BF16)
    sgt = SB(es2, "sgt", [128, 2, 512], F32)
    carry = SB(es2, "carry", [128, 2, 44, 2], BF16)
    uT = SB(es2, "uT", [128, 8, 30 + 512], BF16)
    cT = SB(es2, "cT", [128, 8, 512], F32)
    cb16 = SB(es2, "cb16", [128, 8, 512], BF16)
    lnm = SB(es2, "lnm", [128, 3, 512], F32)
    sT = SB(es2, "sT", [128, 8, 512], BF16)
    ones_row = SB(es2, "ones_row", [1, 128], BF16)
    bias_row = SB(es2, "bias_row", [1, D], BF16)
    pst2 = SB(es2, "pst2", [128, 4, 4], F32)
    pntmp2 = SB(es2, "pntmp2", [128, 512], F32)
    gbc = SB(es2, "gbc", [128, 4, D], F32)
    if _os.environ.get("KDEBUG"):
        print("sweep2 sbuf bytes remaining:", nc.sbuf_bytes_remaining)
    for i in range(4):
        DMA("sp", gbc[:, i, :], bc_d[i + 1].partition_broadcast(128), [], [("gbc", i)], ("c1", i))
    P.emit("pool", lambda e: e.memset(carry[:], 0.0), [], ["carry"])
    P.emit("pool", lambda e: e.memset(corr[:], 0.0), [], ["corr"])
    P.emit("pool", lambda e: e.memset(ones_row[:], 1.0), [], ["ones_row"])
    P.emit("pool", lambda e: e.dma_start(out=bias_row[:], in_=bc_d[4:5, :]), [], ["bias_row"], dma="c1_bias")
    P.emit("pool", lambda e: e.memset(uT[:], 0.0), [], ["uT"])
    bank_state.update(n=0, ring=list(range(8)))

    ffn_state = dict(par=0)

    def ffn_norm(L, nb, xr, xr_res_f, halo):
        for b in range(nb):
            norm_to_T(xr[:, b, :], xr_res_f(b), PP_GT + 8 * (1 + 2 * L), hn2, junk2[:], nst2, b,
                      hnT2[:, :, b * 128:(b + 1) * 128], ("hnT2", b),
                      extra_scale=(flag[:, 0:1] if (halo and L == 1) else None))

    def ffn(L, nt, nb, xr, xr_res_f, gidx, halo, skip_norm=False, pre_down_hook=None, next_norm=None):
        if not skip_norm:
            ffn_norm(L, nb, xr, xr_res_f, halo)
        hres = [("hnT2", b) for b in range(nb)]
        pend = None
        Wcur = [None, None]
        par = ffn_state["par"]
        for j in range(23):
            kb = None
            if j < 22:
                i, pl = j // 2, j % 2
                if pl == 0:
                    Wap, Wres = wnext(("up", L, i))
                    Wcur = [Wap.rearrange("p (kc u n) -> p kc u n", kc=8, u=2), Wres]
                W4, Wres = Wcur
                kb = []
                for ug in range(2):
                    k = mmbank()
                    for kc in range(8):
                        MM(ps[k][:, 0:nt], W4[:, kc, ug, pl * 128:(pl + 1) * 128], hnT2[:, kc, 0:nt], kc == 0, kc == 7,
                           hres + [Wres], [("ps", k)])
                    kb.append(k)
                ch = j
                sl = j % 2
                wc = PP_FW + (L * 44 + ch) * 3
                ACT(zsb[:, sl, 2:2 + nt], ps[kb[0]][:, 0:nt], AF.Copy, [("ps", kb[0])], [("zsb", sl)])
                cres = ("carry", L, ch)
                CP("pool", zsb[:, sl, 0:2], carry[:, L, ch, :], ["carry", cres], [("zsbc", sl)])
                CP("pool", carry[:, L, ch, :], zsb[:, sl, nt:nt + 2], [("zsb", sl)], [cres])
                for tap in range(3):
                    ACT(dgf[:, sl, tap, :], identB[:], AF.Identity, ["identB", "pp"], [("dgf", sl)], scale=pp[:, wc + tap:wc + tap + 1])
                ch = 22 + j
                wc = PP_FW + (L * 44 + ch) * 3
                bcol = PP_FB + L * 44 + ch
                z = ps[kb[1]]
                ares = ("cacc", sl)
                zr = [("ps", kb[1])]
                TS("dve", cacc[:, sl, 0:nt], z[:, 0:nt], pp[:, wc + 2:wc + 3], pp[:, bcol:bcol + 1], ALU.mult, ALU.add, zr + ["pp"], [ares])
                STT(cacc[:, sl, 1:nt], z[:, 0:nt - 1], pp[:, wc + 1:wc + 2], cacc[:, sl, 1:nt], ALU.mult, ALU.add, zr, [ares])
                STT(cacc[:, sl, 2:nt], z[:, 0:nt - 2], pp[:, wc:wc + 1], cacc[:, sl, 2:nt], ALU.mult, ALU.add, zr, [ares])
                TT("dve", cacc[:, sl, 0:2], cacc[:, sl, 0:2], corr[:, par, L, j, :], ALU.add, ["corr", ("corr", par, L, j)], [ares])
                nres = ("corr", 1 - par, L, j)
                TS("dve", corr[:, 1 - par, L, j, :], z[:, nt - 2:nt], pp[:, wc:wc + 1], None, ALU.mult, None, zr + ["corr"], [nres])
                STT(corr[:, 1 - par, L, j, 0:1], z[:, nt - 1:nt], pp[:, wc + 1:wc + 2], corr[:, 1 - par, L, j, 0:1], ALU.mult, ALU.add,
                    zr, [nres])
                ACT(sgt[:, sl, 0:nt], cacc[:, sl, 0:nt], AF.Silu, [ares], [("sgt", sl)])
            if pend is not None:
                pj, pkb = pend
                sl = pj % 2
                k = mmbank()
                for tap in range(3):
                    MM(ps[k][:, 0:nt], dgf[:, sl, tap, :], zsb[:, sl, tap:tap + nt], tap == 0, tap == 2,
                       [("zsb", sl), ("zsbc", sl), ("dgf", sl)], [("ps", k)])
                bu = PP_FB + L * 44 + pj
                STT(hT[:, pj, 0:nt], ps[k][:, 0:nt], pp[:, bu:bu + 1], sgt[:, sl, 0:nt], ALU.add, ALU.mult,
                    [("ps", k), ("sgt", sl), "pp"], [("hT", pj)])
            pend = (j, kb) if j < 22 else None
        if L == 1:
            ffn_state["par"] = 1 - par
        if pre_down_hook is not None:
            pre_down_hook()
        ybank = {}
        for hf in range(2):
            for b in range(nb):
                ybank[(b, hf)] = mmbank()
            for pc in range(3):
                Wap, Wres = wnext(("dn", L, hf, pc))
                W3 = Wap.rearrange("p (kc n) -> p kc n", kc=8)
                k0, k1 = [(0, 8), (8, 16), (16, 22)][pc]
                for b in range(nb):
                    k = ybank[(b, hf)]
                    for kc in range(k0, k1):
                        MM(ps[k][:], hT[:, kc, b * 128:(b + 1) * 128], W3[:, kc - k0, :], kc == 0, kc == 21,
                           [("hT", kc), Wres], [("ps", k)])
        if next_norm is not None:
            postnorm_then_norm(nb, [(ybank[(b, 0)], ybank[(b, 1)]) for b in range(nb)], gbc[:, gidx, :], ("gbc", gidx),
                               xr, xr_res_f, next_norm[0], next_norm[1])
        else:
            for b in range(nb):
                k0_, k1_ = ybank[(b, 0)], ybank[(b, 1)]
                postnorm_resid([ps[k0_][:], ps[k1_][:]], [("ps", k0_), ("ps", k1_)], gbc[:, gidx, :], ("gbc", gidx),
                               xr[:, b, :], xr_res_f(b), pst2, b, junk2, pntmp2)

    def conformer(nt, nb, xr, xr_res_f, halo, nt_prev, skip_norm=False, next_norm=None):
        for b in (range(nb) if not skip_norm else []):
            norm_to_T(xr[:, b, :], xr_res_f(b), PP_GT + 16, hn2, junk2[:], nst2, b,
                      hnT2[:, :, b * 128:(b + 1) * 128], ("hnT2", b))
        hres = [("hnT2", b) for b in range(nb)]
        if nt_prev is not None:
            CP("pool", uT[:, :, 0:30], uT[:, :, nt_prev:nt_prev + 30], ["uT"] + [("uT", c) for c in range(8)], ["uTc"])
        for i in range(4):
            Wap, Wres = wnext(("pw1", i))
            W4 = Wap.rearrange("p (kc u n) -> p kc u n", kc=8, u=2)
            for pl in range(2):
                c = 2 * i + pl
                kb = []
                for ug in range(2):
                    k = mmbank()
                    for kc in range(8):
                        MM(ps[k][:, 0:nt], W4[:, kc, ug, pl * 128:(pl + 1) * 128], hnT2[:, kc, 0:nt], kc == 0, kc == 7,
                           hres + [Wres], [("ps", k)])
                    kb.append(k)
                ACT(sgt[:, c % 2, 0:nt], ps[kb[1]][:, 0:nt], AF.Sigmoid, [("ps", kb[1]), "pp"], [("sgt", c % 2)],
                    bias=pp[:, PP_BPW1 + 8 + c:PP_BPW1 + 9 + c])
                STT(uT[:, c, 30:30 + nt], ps[kb[0]][:, 0:nt], pp[:, PP_BPW1 + c:PP_BPW1 + c + 1], sgt[:, c % 2, 0:nt],
                    ALU.add, ALU.mult, [("ps", kb[0]), ("sgt", c % 2), "uTc", "uT"], [("uT", c)])
                if halo:
                    TS("dve", uT[:, c, 30:30 + nt], uT[:, c, 30:30 + nt], flag[:, 0:1], None, ALU.mult, None, ["flag"], [("uT", c)])
        NPE = 23
        for j in range(NPE, 31):
            for c in range(8):
                wc = PP_WDW + c * 31
                if j == NPE:
                    TS("dve", cT[:, c, 0:nt], uT[:, c, j:j + nt], pp[:, wc + j:wc + j + 1], None, ALU.mult, None,
                       [("uT", c), "uTc", "pp"], [("cT", c)])
                else:
                    STT(cT[:, c, 0:nt], uT[:, c, j:j + nt], pp[:, wc + j:wc + j + 1], cT[:, c, 0:nt], ALU.mult, ALU.add,
                        [("uT", c), "uTc", "pp"], [("cT", c)])
        for c in range(8):
            wc = PP_WDW + c * 31
            for j in range(NPE):
                if j % 2 == 0:
                    TS("dve", dg31[:, c % 2, j, :], identB[:], pp[:, wc + j:wc + j + 1], None, ALU.mult, None,
                       ["identB", "pp"], [("dg31", c % 2, j)])
                else:
                    ACT(dg31[:, c % 2, j, :], identB[:], AF.Identity, ["identB", "pp"], [("dg31", c % 2, j)],
                        scale=pp[:, wc + j:wc + j + 1])
            k = mmbank()
            for j in range(NPE):
                MM(ps[k][:, 0:nt], dg31[:, c % 2, j, :], uT[:, c, j:j + nt], j == 0, j == NPE - 1,
                   [("uT", c), "uTc", ("dg31", c % 2, j)], [("ps", k)])
            STT(cT[:, c, 0:nt], ps[k][:, 0:nt], pp[:, PP_BDW + c:PP_BDW + c + 1], cT[:, c, 0:nt], ALU.add, ALU.add,
                [("ps", k), "pp"], [("cT", c)])
        km, kq = mmbank(), mmbank()
        for c in range(8):
            ACT(cb16[:, c, 0:nt], cT[:, c, 0:nt], AF.Copy, [("cT", c)], [("cb16", c)])
        for c in range(8):
            MM(ps[km][:, 0:nt], onesdiv[:], cb16[:, c, 0:nt], c == 0, c == 7, [("cb16", c), "onesdiv"], [("ps", km)])
        for c in range(8):
            ACT(cb16[:, c, 0:nt], cT[:, c, 0:nt], AF.Square, [("cT", c)], [("cb16", c)])
        for c in range(8):
            MM(ps[kq][:, 0:nt], onesdiv[:], cb16[:, c, 0:nt], c == 0, c == 7, [("cb16", c), "onesdiv"], [("ps", kq)])
        mean, var, rstd = lnm[:, 0, 0:nt], lnm[:, 1, 0:nt], lnm[:, 2, 0:nt]
        CP("dve", mean, ps[km][:, 0:nt], [("ps", km)], ["ln_mean"])
        TT("dve", var, mean, mean, ALU.mult, ["ln_mean"], ["ln_var"])
        TT("dve", var, ps[kq][:, 0:nt], var, ALU.subtract, [("ps", kq)], ["ln_var"])
        TS("dve", var, var, 0.0, None, ALU.max, None, [], ["ln_var"])
        ACT(rstd, var, AF.Sqrt, ["ln_var"], ["ln_rstd"], bias=EPS)
        P.emit("dve", lambda e: e.reciprocal(out=rstd, in_=rstd), ["ln_rstd"], ["ln_rstd"])
        for c in range(8):
            TT("dve", cT[:, c, 0:nt], cT[:, c, 0:nt], mean, ALU.subtract, ["ln_mean"], [("cT", c)])
            TT("dve", cT[:, c, 0:nt], cT[:, c, 0:nt], rstd, ALU.mult, ["ln_rstd"], [("cT", c)])
            ACT(sT[:, c, 0:nt], cT[:, c, 0:nt], AF.Silu, [("cT", c), "pp"], [("sT", c)],
                scale=pp[:, PP_LNG + c:PP_LNG + c + 1], bias=pp[:, PP_LNB + c:PP_LNB + c + 1])
        ybank = {}
        for hf in range(2):
            Wap, Wres = wnext(("pw2", hf))
            W3 = Wap.rearrange("p (kc n) -> p kc n", kc=8)
            for b in range(nb):
                k = mmbank()
                ybank[(b, hf)] = k
                for kc in range(8):
                    MM(ps[k][:], sT[:, kc, b * 128:(b + 1) * 128], W3[:, kc, :], kc == 0, False, [("sT", kc), Wres], [("ps", k)])
                MM(ps[k][:], ones_row[:], bias_row[:, hf * 512:(hf + 1) * 512], False, True, ["ones_row", "bias_row"], [("ps", k)])
        if next_norm is not None:
            postnorm_then_norm(nb, [(ybank[(b, 0)], ybank[(b, 1)]) for b in range(nb)], gbc[:, 1, :], ("gbc", 1),
                               xr, xr_res_f, next_norm[0], next_norm[1])
        else:
            for b in range(nb):
                k0_, k1_ = ybank[(b, 0)], ybank[(b, 1)]
                postnorm_resid([ps[k0_][:], ps[k1_][:]], [("ps", k0_), ("ps", k1_)], gbc[:, 1, :], ("gbc", 1),
                               xr[:, b, :], xr_res_f(b), pst2, b, junk2, pntmp2)

    tiles2 = [dict(B0=HALO_BLK, nb=1, halo=True)] + [dict(B0=20 + 4 * j, nb=4, halo=False) for j in range(8)]

    def tile_ctx(ti):
        tl = tiles2[ti]
        xs_ = ti % 2
        return tl, XR[:, xs_, :, :], (lambda b, xs_=xs_: ("XR", xs_, b)), xs_

    def load_and_norm(ti):
        tl, xr, xr_res_f, xs_ = tile_ctx(ti)
        for b in range(tl["nb"]):
            DMA("sp", xr[:, b, :], x1_d[(tl["B0"] + b - HALO_BLK) * 128:(tl["B0"] + b - HALO_BLK + 1) * 128, :],
                [("x1d", tl["B0"] + b)], [xr_res_f(b)], ("xr", xs_, b))
        ffn_norm(0, tl["nb"], xr, xr_res_f, tl["halo"])

    nt_prev = None
    load_and_norm(0)
    for ti in range(len(tiles2)):
        tl, xr, xr_res_f, xs_ = tile_ctx(ti)
        nb, B0 = tl["nb"], tl["B0"]
        nt = nb * 128
        ffn(0, nt, nb, xr, xr_res_f, 0, tl["halo"], skip_norm=True, next_norm=(PP_GT + 16, None))
        conformer(nt, nb, xr, xr_res_f, tl["halo"], nt_prev, skip_norm=True,
                  next_norm=(PP_GT + 24, flag[:, 0:1] if tl["halo"] else None))
        hook = (lambda ti=ti: load_and_norm(ti + 1)) if ti + 1 < len(tiles2) else None
        ffn(1, nt, nb, xr, xr_res_f, 2, tl["halo"], skip_norm=True, pre_down_hook=hook)
        nt_prev = nt
        if not tl["halo"]:
            for b in range(nb):
                r0 = (B0 - 20 + b) * 128
                DMA("sp", out_d[r0:r0 + 128, :], xr[:, b, :], [xr_res_f(b)], [("outd", r0)], ("out", xs_, b))
    P.ins["sp"].append(dict(fn=None, waits=[("d", k, v) for k, v in P.dma_count.items()
                                            if (isinstance(k, tuple) and k[0] == "out") or k == "x1out"], dma=None))
    assert wstate["used"] == len(wseq)

    with ExitStack() as esm:
        esem = {e: esm.enter_context(nc.semaphore("sem_" + e)) for e in Prog.ENG}
        dsem = {}
        for i, k in enumerate(sorted(P.dma_count.keys(), key=str)):
            dsem[k] = esm.enter_context(nc.semaphore(f"dsem{i}"))
        block = esm.enter_context(nc.Block())
        P.replay(nc, block, esem, dsem)
    es2.close()
    es0.close()
    return nc


def _host_prep(inputs):
    f = np.float32
    x = np.asarray(inputs["x"], f)
    pos = np.asarray(inputs["positions"]).astype(np.int32)
    g = np.asarray(inputs["norm_g"], f)
    pp = np.zeros((128, PP_N), f)

    def fm(v):
        return np.asarray(v, f).reshape(-1, 128).T

    for i, (L, j) in enumerate([(0, 0), (0, 2), (1, 0), (1, 2)]):
        pp[:, PP_GT + 8 * i:PP_GT + 8 * i + 8] = fm(g[L, j])
    pp[:, PP_BPW1:PP_BPW1 + 16] = fm(inputs["conv_b_pw1"][0])
    wdw = np.asarray(inputs["conv_w_dw"][0], f)
    pp[:, PP_WDW:PP_WDW + 248] = wdw.reshape(31, 8, 128).transpose(2, 1, 0).reshape(128, 248)
    pp[:, PP_BDW:PP_BDW + 8] = fm(inputs["conv_b_dw"][0])
    pp[:, PP_LNG:PP_LNG + 8] = fm(inputs["conv_ln_g"][0])
    pp[:, PP_LNB:PP_LNB + 8] = fm(inputs["conv_ln_b"][0])
    fw = np.asarray(inputs["ffn_w_dw"], f)
    pp[:, PP_FW:PP_FW + 264] = fw.reshape(2, 3, 44, 128).transpose(3, 0, 2, 1).reshape(128, 264)
    fb = np.asarray(inputs["ffn_b_dw"], f)
    pp[:, PP_FB:PP_FB + 88] = fb.reshape(2, 44, 128).transpose(2, 0, 1).reshape(128, 88)
    bc = np.stack([g[0, 1], g[0, 3], g[1, 1], g[1, 3], np.asarray(inputs["conv_b_pw2"][0], f)]).astype(f)
    cst = np.zeros((128, C_N), f)
    cst[:, C_ID:C_ID + 128] = np.eye(128, dtype=f)
    cidx = np.arange(128)[:, None]
    aidx = np.arange(128)[None, :]
    cst[:, C_MASK:C_MASK + 128] = (cidx >= aidx).astype(f)
    cst[:, C_MASK + 128:C_MASK + 256] = (cidx <= aidx).astype(f)
    half = 8
    invf = (500000.0 ** (-np.arange(half, dtype=np.float32) / half)).astype(f)
    cst[:, C_INVF:C_INVF + 8] = invf[None, :]
    shared = dict(
        w_qkv=np.ascontiguousarray(inputs["attn_w_qkv"][0], dtype=f),
        w_o=np.ascontiguousarray(inputs["attn_w_o"][0], dtype=f),
        w_pw1=np.ascontiguousarray(inputs["conv_w_pw1"][0], dtype=f),
        w_pw2=np.ascontiguousarray(inputs["conv_w_pw2"][0], dtype=f),
        w_up=np.ascontiguousarray(inputs["ffn_w_up"], dtype=f),
        w_down=np.ascontiguousarray(inputs["ffn_w_down"], dtype=f),
        pp=pp, bc=bc, cst=cst)
    in_maps = []
    for c in range(NCORES):
        b, hf = c // 2, c % 2
        s0 = hf * 4096
        lo = s0 - HALO
        xs = np.zeros((NLOC, D), f)
        ps_ = np.zeros((NLOC,), np.int32)
        tmk = np.zeros((NLOC,), f)
        a = max(lo, 0)
        xs[a - lo:] = x[b, a:s0 + 4096]
        ps_[a - lo:] = pos[b, a:s0 + 4096]
        tmk[a - lo:] = 1.0
        m = dict(shared)
        m["xs"] = xs
        m["pos"] = np.ascontiguousarray(ps_.reshape(NBLK, 128).T)
        m["tmask"] = np.ascontiguousarray(tmk.reshape(NBLK, 128).T)
        m["flag"] = np.full((128, 1), float(hf), f)
        in_maps.append(m)
    return in_maps


_NC_CACHE = {}


def kernel(**inputs):
    in_maps = _host_prep(inputs)
    if "nc" not in _NC_CACHE:
        _NC_CACHE["nc"] = build_program()
    nc = _NC_CACHE["nc"]
    res = run_bass_kernel_spmd(nc, in_maps, core_ids=list(range(NCORES)))
    out = np.zeros((NB, SEQ, D), np.float32)
    for c in range(NCORES):
        b, hf = c // 2, c % 2
        out[b, hf * 4096:(hf + 1) * 4096] = res.results[c]["out"]
    return out
```

```python
import numpy as np
from contextlib import ExitStack
import concourse.bass as bass
import concourse.mybir as mybir
from concourse.bass_utils import run_bass_kernel_spmd

F32 = mybir.dt.float32
BF16 = mybir.dt.bfloat16
I32 = mybir.dt.int32
AF = mybir.ActivationFunctionType
ALU = mybir.AluOpType

D = 1024
SEQ = 8192
NB = 4
NCORES = 8
HALO = 2560
NLOC = HALO + 4096
NBLK = NLOC // 128
NTILE = NLOC // 512
HALO_BLK = 19
FFN = 2816
EPS = 1e-6
DIL = {1: 1, 2: 4, 3: 16}
NSLOT = 3

PP_GT = 0
PP_BPW1 = PP_GT + 32
PP_WDW = PP_BPW1 + 16
PP_BDW = PP_WDW + 248
PP_LNG = PP_BDW + 8
PP_LNB = PP_LNG + 8
PP_FW = PP_LNB + 8
PP_FB = PP_FW + 264
PP_N = PP_FB + 88
C_ID = 0
C_MASK = 128
C_INVF = C_MASK + 256
C_N = C_INVF + 8


class Prog:
    ENG = ["pe", "act", "dve", "pool", "sp"]

    def __init__(self):
        self.ins = {e: [] for e in self.ENG}
        self.lastw = {}
        self.rd = {}
        self.dma_count = {}

    def emit(self, eng, fn, reads=(), writes=(), dma=None):
        deps = set()
        for r in reads:
            if r in self.lastw:
                deps.add(self.lastw[r])
            if isinstance(r, tuple) and r[0] == "ps":
                for k, v in self.rd.get(r, {}).items():
                    if k[1] != eng:
                        deps.add(k + (v,))
        for w in writes:
            if w in self.lastw:
                deps.add(self.lastw[w])
            for k, v in self.rd.get(w, {}).items():
                deps.add(k + (v,))
        idx = len(self.ins[eng])
        if dma is None:
            ev = ("e", eng, idx)
        else:
            self.dma_count[dma] = self.dma_count.get(dma, 0) + 16
            ev = ("d", dma, self.dma_count[dma])
        waits = [d for d in deps if not (d[0] == "e" and d[1] == eng and eng in ("pe", "sp"))]
        self.ins[eng].append(dict(fn=fn, waits=waits, dma=dma))
        for r in reads:
            m = self.rd.setdefault(r, {})
            k = ev[:2]
            m[k] = max(m.get(k, -1), ev[2])
        for w in writes:
            self.lastw[w] = ev
            self.rd[w] = {}
        return ev

    def barrier(self, dkeys=None):
        last = {}
        for e in self.ENG:
            for i in range(len(self.ins[e]) - 1, -1, -1):
                if self.ins[e][i]["fn"] is not None and self.ins[e][i]["dma"] is None:
                    last[e] = i
                    break
        dm = {k: v for k, v in self.dma_count.items() if dkeys is None or k in dkeys}
        for e in self.ENG:
            waits = [("e", o, i) for o, i in last.items() if o != e] + [("d", k, v) for k, v in dm.items()]
            self.ins[e].append(dict(fn=None, waits=waits, dma=None))

    def replay(self, nc, block, esem, dsem):
        sig = {e: set() for e in self.ENG}
        for e in self.ENG:
            for it in self.ins[e]:
                for w in it["waits"]:
                    if w[0] == "e":
                        sig[w[1]].add(w[2])
        rank = {e: {i: r + 1 for r, i in enumerate(sorted(sig[e]))} for e in self.ENG}

        def run(eng, eo):
            known = {}
            for i, it in enumerate(self.ins[eng]):
                need = {}
                for w in it["waits"]:
                    if w[0] == "e":
                        key, val = ("e", w[1]), rank[w[1]][w[2]]
                    else:
                        key, val = ("d", w[1]), w[2]
                    if known.get(key, 0) < val:
                        need[key] = max(need.get(key, 0), val)
                for key, val in need.items():
                    eo.wait_ge(esem[key[1]] if key[0] == "e" else dsem[key[1]], val)
                    known[key] = val
                if it["fn"] is None:
                    continue
                bi = it["fn"](eo)
                if it["dma"] is not None:
                    bi.then_inc(dsem[it["dma"]], 16)
                elif i in rank[eng]:
                    bi.then_inc(esem[eng], 1)

        @block.tensor
        def _(e):
            run("pe", e)

        @block.scalar
        def _(e):
            run("act", e)

        @block.vector
        def _(e):
            run("dve", e)

        @block.gpsimd
        def _(e):
            run("pool", e)

        @block.sync
        def _(e):
            run("sp", e)


def chunk_plan():
    ids = {}
    n = 0
    for nm in ["v3", "v2", "v1", "k3", "k2", "k1", "q3", "q2", "q1", "wo"]:
        ids[nm] = n
        n += 1
    for L in range(2):
        for i in range(11):
            ids[("up", L, i)] = n
            n += 1
        for hf in range(2):
            for pc in range(3):
                ids[("dn", L, hf, pc)] = n
                n += 1
    for i in range(4):
        ids[("pw1", i)] = n
        n += 1
    for hf in range(2):
        ids[("pw2", hf)] = n
        n += 1
    return ids, n


def s1_kv_groups(t):
    return [3] if t <= 2 else ([3, 2] if t == 3 else [3, 2, 1])


def s1_tile_chunks(t):
    kvg = s1_kv_groups(t)
    ch = ["v%d" % g for g in kvg] + ["k%d" % g for g in kvg]
    if t >= 4:
        ch += ["q3", "q2", "q1", "wo"]
    return ch


def s2_tile_chunks():
    seq = []
    seq += [("up", 0, i) for i in range(11)]
    seq += [("dn", 0, hf, pc) for hf in range(2) for pc in range(3)]
    seq += [("pw1", i) for i in range(4)]
    seq += [("pw2", hf) for hf in range(2)]
    seq += [("up", 1, i) for i in range(11)]
    seq += [("dn", 1, hf, pc) for hf in range(2) for pc in range(3)]
    return seq


def build_program(debug=False, phase=2):
    nc = bass.Bass("TRN2", target_bir_lowering=False)
    P = Prog()

    def din(name, shape, dt=F32):
        return nc.dram_tensor(name, shape, dt, kind="ExternalInput").ap()

    xs_d = din("xs", [NLOC, D])
    pos_d = din("pos", [128, NBLK], I32)
    tm_d = din("tmask", [128, NBLK])
    flag_d = din("flag", [128, 1])
    wqkv_d = din("w_qkv", [D, 4608])
    wo_d = din("w_o", [512, D])
    wpw1_d = din("w_pw1", [D, 2048])
    wpw2_d = din("w_pw2", [D, D])
    wup_d = din("w_up", [2, D, 2 * FFN])
    wdn_d = din("w_down", [2, FFN, D])
    pp_d = din("pp", [128, PP_N])
    bc_d = din("bc", [5, D])
    cst_d = din("cst", [128, C_N])
    out_d = nc.dram_tensor("out", [4096, D], F32, kind="ExternalOutput").ap()
    x1_kind = "ExternalOutput" if debug else "Internal"
    x1_d = nc.dram_tensor("x1s", [33 * 128, D], F32, kind=x1_kind).ap()
    cids, NCH = chunk_plan()
    W_d = nc.dram_tensor("wbf", [NCH, 128, 4096], BF16, kind="Internal").ap()
    Vd = {g: nc.dram_tensor(f"vd{g}", [NLOC, 520], BF16, kind="Internal").ap() for g in (2, 3)}

    es0 = ExitStack()
    E0 = es0.enter_context

    def SB(stack, name, shape, dt):
        return stack.enter_context(nc.sbuf_tensor("sb_" + name, shape, dt))

    identF = SB(es0, "identF", [128, 128], F32)
    identB = SB(es0, "identB", [128, 128], BF16)
    maskUL = SB(es0, "maskUL", [128, 2, 128], BF16)
    onesdiv = SB(es0, "onesdiv", [128, 128], BF16)
    invf = SB(es0, "invf", [128, 8], F32)
    pp = SB(es0, "pp", [128, PP_N], F32)
    tm = SB(es0, "tm", [128, NBLK], F32)
    flag = SB(es0, "flag", [128, 1], F32)
    wsl = SB(es0, "wsl", [128, NSLOT, 4096], BF16)
    ps = [E0(nc.psum_tensor(f"ps{k}", [128, 512], F32)) for k in range(8)]
    psb = [p[:].bitcast(BF16) for p in ps]

    def prekey(nm):
        if nm in ("v3", "k3"):
            return "pre0"
        if nm in ("v2", "v1", "k2", "k1"):
            return "pre1"
        if nm in ("q3", "q2", "q1", "wo"):
            return "pre2"
        if isinstance(nm, tuple) and nm[0] in ("up", "dn") and nm[1] == 0:
            return "pre3"
        return "pre4"

    import os as _os
    _nopre = bool(_os.environ.get("NOPRE"))

    _pre_list = []

    def pre_dma(nm, out_ap, in_ap):
        _pre_list.append((nm, out_ap, in_ap))

    def prepass_some(n):
        if not _pre_list:
            return
        pw = [("d", k, v) for k, v in P.dma_count.items() if str(k).startswith("pre")]
        if pw:
            P.ins["pool"].append(dict(fn=None, waits=pw, dma=None))
        for _ in range(min(n, len(_pre_list))):
            nm, o, i = _pre_list.pop(0)
            P.emit("pool", lambda e, o=o, i=i: e.dma_start(out=o, in_=i), reads=(), writes=[("wd", cids[nm])], dma=prekey(nm))

    def kcview(ap2d):
        return ap2d.rearrange("(kc p) n -> p kc n", p=128)

    cg_col = {"q1": 0, "q2": 1, "q3": 2, "k1": 3, "k2": 4, "k3": 5, "v1": 6, "v2": 7, "v3": 8}
    order = ["v3", "k3", "v2", "v1", "k2", "k1", "q3", "q2", "q1", "wo"]
    for nm in order:
        ch = W_d[cids[nm]]
        if nm == "wo":
            pre_dma(nm, ch.rearrange("p (kc n) -> p kc n", kc=4), kcview(wo_d))
        else:
            c0 = cg_col[nm] * 512
            pre_dma(nm, ch.rearrange("p (kc n) -> p kc n", kc=8), kcview(wqkv_d[:, c0:c0 + 512]))
    for nm in s2_tile_chunks():
        ch = W_d[cids[nm]]
        if nm[0] == "up":
            _, L, i = nm
            v = ch.rearrange("p (kc u n) -> p kc u n", kc=8, u=2)
            pre_dma(nm, v[:, :, 0, :], kcview(wup_d[L][:, 256 * i:256 * i + 256]))
            pre_dma(nm, v[:, :, 1, :], kcview(wup_d[L][:, FFN + 256 * i:FFN + 256 * i + 256]))
        elif nm[0] == "dn":
            _, L, hf, pc = nm
            k0, k1 = [(0, 8), (8, 16), (16, 22)][pc]
            v = ch.rearrange("p (kc n) -> p kc n", kc=8)
            pre_dma(nm, v[:, 0:k1 - k0, :], kcview(wdn_d[L][k0 * 128:k1 * 128, hf * 512:hf * 512 + 512]))
        elif nm[0] == "pw1":
            i = nm[1]
            v = ch.rearrange("p (kc u n) -> p kc u n", kc=8, u=2)
            pre_dma(nm, v[:, :, 0, :], kcview(wpw1_d[:, 256 * i:256 * i + 256]))
            pre_dma(nm, v[:, :, 1, :], kcview(wpw1_d[:, 1024 + 256 * i:1024 + 256 * i + 256]))
        else:
            hf = nm[1]
            pre_dma(nm, ch.rearrange("p (kc n) -> p kc n", kc=8), kcview(wpw2_d[:, hf * 512:hf * 512 + 512]))
    def prepass_fix(keys):
        for nm, cid in cids.items():
            k = prekey(nm)
            if k in keys:
                P.lastw[("wd", cid)] = ("d", k, P.dma_count[k])

    prepass_some(5)
    prepass_some(5)
    prepass_fix(("pre0", "pre1", "pre2"))

    wseq = []
    for t in range(NTILE):
        wseq += s1_tile_chunks(t)
    for _ in range(9):
        wseq += s2_tile_chunks()
    wstate = dict(issued=0, used=0)

    def wnext(expect):
        n = wstate["used"]
        assert wseq[n] == expect, (n, wseq[n], expect)
        while wstate["issued"] < min(len(wseq), n + NSLOT):
            m = wstate["issued"]
            s = m % NSLOT
            nv = 3072 if (isinstance(wseq[m], tuple) and wseq[m][0] == "dn" and wseq[m][3] == 2) else 4096
            P.emit("sp", lambda e, o=wsl[:, s, 0:nv], i=W_d[cids[wseq[m]]][:, 0:nv]: e.dma_start(out=o, in_=i),
                   reads=[("wd", cids[wseq[m]])], writes=[("w", s)], dma=("w", s))
            wstate["issued"] += 1
        wstate["used"] += 1
        s = n % NSLOT
        return wsl[:, s, :], ("w", s)

    def ACT(out, in_, func, reads, writes, **kw):
        return P.emit("act", lambda e: e.activation(out=out, in_=in_, func=func, **kw), reads, writes)

    def TT(eng, out, in0, in1, op, reads, writes):
        return P.emit(eng, lambda e: e.tensor_tensor(out=out, in0=in0, in1=in1, op=op), reads, writes)

    def TS(eng, out, in0, s1, s2, op0, op1, reads, writes):
        if s2 is None:
            return P.emit(eng, lambda e: e.tensor_scalar(out=out, in0=in0, scalar1=s1, scalar2=None, op0=op0), reads, writes)
        return P.emit(eng, lambda e: e.tensor_scalar(out=out, in0=in0, scalar1=s1, scalar2=s2, op0=op0, op1=op1), reads, writes)

    def STT(out, in0, scalar, in1, op0, op1, reads, writes):
        return P.emit("dve", lambda e: e.scalar_tensor_tensor(out=out, in0=in0, scalar=scalar, in1=in1, op0=op0, op1=op1),
                      reads, writes)

    def CP(eng, out, in_, reads, writes):
        if eng == "act":
            return P.emit("act", lambda e: e.activation(out=out, in_=in_, func=AF.Copy), reads, writes)
        return P.emit(eng, lambda e: e.tensor_copy(out=out, in_=in_), reads, writes)

    def MM(out, lhsT, rhs, start, stop, reads, writes):
        return P.emit("pe", lambda e: e.matmul(out, lhsT=lhsT, rhs=rhs, start=start, stop=stop), reads, writes)

    def TR(out, in_, ident, reads, writes):
        return P.emit("pe", lambda e: e.transpose(out, in_, ident), reads, writes)

    def DMA(eng, out, in_, reads, writes, key):
        return P.emit(eng, lambda e: e.dma_start(out=out, in_=in_), reads, writes, dma=key)

    bank_state = dict(n=0, ring=[0, 1, 2, 3])

    def mmbank():
        r = bank_state["ring"]
        k = r[bank_state["n"] % len(r)]
        bank_state["n"] += 1
        return k

    cosT = SB(es0, "cosT", [128, NBLK, 8], F32)
    sinT = SB(es0, "sinT", [128, NBLK, 8], F32)
    esi = ExitStack()
    cstf = SB(esi, "cstf", [128, C_N], F32)
    DMA("sp", cstf[:], cst_d, [], ["cstf"], "c0_1")
    DMA("sp", pp[:], pp_d, [], ["pp"], "c0_2")
    DMA("sp", tm[:], tm_d, [], ["tm"], "c0_3")
    DMA("sp", flag[:], flag_d, [], ["flag"], "c0_4")
    CP("dve", identF[:], cstf[:, C_ID:C_ID + 128], ["cstf"], ["identF"])
    CP("dve", identB[:], cstf[:, C_ID:C_ID + 128], ["cstf"], ["identB"])
    CP("dve", maskUL[:], cstf[:, C_MASK:C_MASK + 256].rearrange("p (u n) -> p u n", u=2), ["cstf"], ["maskUL"])
    CP("dve", invf[:], cstf[:, C_INVF:C_INVF + 8], ["cstf"], ["invf"])
    P.emit("dve", lambda e: e.memset(onesdiv[:], 1.0 / 1024.0), [], ["onesdiv"])

    if True:
        esr = esi
        posi = SB(esr, "posi", [128, NBLK], I32)
        posf = SB(esr, "posf", [128, NBLK], F32)
        ang = SB(esr, "ang", [128, NBLK, 8], F32)
        ang2 = SB(esr, "ang2", [128, NBLK, 8], F32)
        kf = SB(esr, "kf", [128, NBLK, 8], F32)
        ki = SB(esr, "ki", [128, NBLK, 8], I32)
        DMA("sp", posi[:], pos_d, [], ["posi"], "c0_5")
        CP("dve", posf[:], posi[:], ["posi"], ["posf"])
        TT("dve", ang[:], posf[:].unsqueeze(2).to_broadcast([128, NBLK, 8]),
           invf[:].unsqueeze(1).to_broadcast([128, NBLK, 8]), ALU.mult, ["posf", "invf"], ["ang"])
        C1 = 6.28125
        C2 = float(2.0 * np.pi - 6.28125)
        for (dst, shift) in ((sinT, 0.0), (cosT, float(np.pi / 2))):
            src = ang
            if shift != 0.0:
                TS("dve", ang2[:], ang[:], shift, None, ALU.add, None, ["ang"], ["ang2"])
                src = ang2
            sname = "ang" if src is ang else "ang2"
            TS("dve", kf[:], src[:], float(1.0 / (2.0 * np.pi)), None, ALU.mult, None, [sname], ["kf"])
            CP("dve", ki[:], kf[:], ["kf"], ["ki"])
            CP("dve", kf[:], ki[:], ["ki"], ["kf"])
            STT(dst[:], kf[:], -C1, src[:], ALU.mult, ALU.add, ["kf", sname], [("rt", id(dst))])
            STT(dst[:], kf[:], -C2, dst[:], ALU.mult, ALU.add, ["kf"], [("rt", id(dst))])
            TS("dve", dst[:], dst[:], float(np.pi), float(-np.pi), ALU.min, ALU.max, [], [("rt", id(dst))])
            ACT(dst[:], dst[:], AF.Sin, [("rt", id(dst))], [("rt", id(dst))])
    ROPE_RES = [("rt", id(sinT)), ("rt", id(cosT))]
    P.barrier(dkeys=[k for k in P.dma_count if str(k).startswith("c0")])
    esi.close()
    if phase == 0:
        P.barrier()
        with ExitStack() as esm:
            esem = {e: esm.enter_context(nc.semaphore("sem_" + e)) for e in Prog.ENG}
            dsem = {}
            for i, k in enumerate(sorted(P.dma_count.keys(), key=str)):
                dsem[k] = esm.enter_context(nc.semaphore(f"dsem{i}"))
            block = esm.enter_context(nc.Block())
            P.replay(nc, block, esem, dsem)
        es0.close()
        return nc

    def rstd_from_ss(ss_ap, rs_ap, reads, res):
        ACT(rs_ap, ss_ap, AF.Sqrt, reads, [res], scale=1.0 / 1024.0, bias=EPS)
        P.emit("dve", lambda e: e.reciprocal(out=rs_ap, in_=rs_ap), [res], [res])

    def n_stats(x_ap, x_res, junk, st, slot):
        sres = ("nst", id(st), slot)
        ACT(junk, x_ap, AF.Square, [x_res], [("junk", id(junk)), sres], accum_out=st[:, slot, 0:1])
        ACT(st[:, slot, 1:2], st[:, slot, 0:1], AF.Sqrt, [sres], [sres], scale=1.0 / 1024.0, bias=EPS)

    def n_scale_tr(x_ap, x_res, hn_tm, st, slot):
        sres = ("nst", id(st), slot)
        rs = st[:, slot, 1:2]
        P.emit("dve", lambda e: e.reciprocal(out=rs, in_=rs), [sres], [sres])
        hres = ("hn_tm", id(hn_tm), slot)
        ACT(hn_tm[:, slot, :], x_ap, AF.Identity, [x_res, sres], [hres], scale=rs)
        k = mmbank()
        for kc in range(8):
            TR(psb[k][:, kc * 128:(kc + 1) * 128], hn_tm[:, slot, kc * 128:(kc + 1) * 128], identB[:],
               [hres, "identB"], [("ps", k)])
        return k

    def n_evac(k, gcol, hnT_dst, hnT_res, extra_scale=None):
        g_b = pp[:, gcol:gcol + 8].unsqueeze(2).to_broadcast([128, 8, 128])
        TT("dve", hnT_dst, psb[k].rearrange("p (kc n) -> p kc n", kc=8), g_b, ALU.mult, [("ps", k), "pp"], [hnT_res])
        if extra_scale is not None:
            TS("dve", hnT_dst, hnT_dst, extra_scale, None, ALU.mult, None, ["flag"], [hnT_res])

    def norm_to_T(x_ap, x_res, gcol, hn_tm, junk, st, slot, hnT_dst, hnT_res, extra_scale=None):
        n_stats(x_ap, x_res, junk, st, slot)
        k = n_scale_tr(x_ap, x_res, hn_tm, st, slot)
        n_evac(k, gcol, hnT_dst, hnT_res, extra_scale)

    def pn_stats(yh, yres, st, slot, junk):
        sres = ("pst", id(st), slot)
        for hf in range(2):
            ACT(junk[:, 0:512], yh[hf], AF.Square, [yres[hf]], [("junk", id(junk)), sres], accum_out=st[:, slot, hf:hf + 1])
        TT("dve", st[:, slot, 2:3], st[:, slot, 0:1], st[:, slot, 1:2], ALU.add, [sres], [sres])
        rstd_from_ss(st[:, slot, 2:3], st[:, slot, 3:4], [sres], sres)

    def pn_apply(yh, yres, gbc_ap, gres, xr_ap, xr_res, st, slot, tmp):
        sres = ("pst", id(st), slot)
        for hf in range(2):
            tr = ("pn_tmp", id(tmp))
            STT(tmp[:, 0:512], yh[hf], st[:, slot, 3:4], gbc_ap[:, hf * 512:(hf + 1) * 512], ALU.mult, ALU.mult,
                [yres[hf], sres, gres], [tr])
            TT("dve", xr_ap[:, hf * 512:(hf + 1) * 512], xr_ap[:, hf * 512:(hf + 1) * 512], tmp[:, 0:512], ALU.add,
               [tr, xr_res], [xr_res])

    def postnorm_resid(yh, yres, gbc_ap, gres, xr_ap, xr_res, st, slot, junk, tmp):
        pn_stats(yh, yres, st, slot, junk)
        pn_apply(yh, yres, gbc_ap, gres, xr_ap, xr_res, st, slot, tmp)

    def postnorm_then_norm(nb, ybanks, gbc_ap, gres, xr, xr_res_f, next_gcol, next_extra):
        for b in range(nb):
            k0_, k1_ = ybanks[b]
            pn_stats([ps[k0_][:], ps[k1_][:]], [("ps", k0_), ("ps", k1_)], pst2, b, junk2)
        trk = {}

        def n_a(b):
            n_stats(xr[:, b, :], xr_res_f(b), junk2[:], nst2, b)

        def n_b(b):
            trk[b] = n_scale_tr(xr[:, b, :], xr_res_f(b), hn2, nst2, b)

        def n_c(b):
            n_evac(trk[b], next_gcol, hnT2[:, :, b * 128:(b + 1) * 128], ("hnT2", b), next_extra)

        for b in range(nb + 2):
            if b < nb:
                k0_, k1_ = ybanks[b]
                pn_apply([ps[k0_][:], ps[k1_][:]], [("ps", k0_), ("ps", k1_)], gbc_ap, gres, xr[:, b, :], xr_res_f(b), pst2, b, pntmp2)
                n_a(b)
            if 1 <= b <= nb:
                n_b(b - 1)
            if 2 <= b:
                n_c(b - 2)

    es1 = ExitStack()
    xsl = SB(es1, "xsl", [128, 2, D], F32)
    junk1 = SB(es1, "junk1", [128, D], BF16)
    hn1 = SB(es1, "hn1", [128, 2, D], BF16)
    nst1 = SB(es1, "nst1", [128, 2, 4], F32)
    hnT = SB(es1, "hnT", [128, 8, 512], BF16)
    stg = SB(es1, "stg", [128, 2, 4, 512], BF16)
    rtmp = SB(es1, "rtmp", [128, 4, 256], F32)
    qT = SB(es1, "qT", [128, 12, 512], BF16)
    kT = {1: SB(es1, "k1T", [128, 4, 640], BF16), 2: SB(es1, "k2T", [128, 4, 1024], BF16),
          3: SB(es1, "k3T", [128, 4, 4096], BF16)}
    V1c = SB(es1, "V1c", [128, 5, 8, 65], BF16)
    V2c = SB(es1, "V2c", [128, 2, 4, 520], BF16)
    V3c = SB(es1, "V3c", [128, 2, 16, 520], BF16)
    Vst = SB(es1, "Vst", [128, 4, 8, 65], BF16)
    Pt = SB(es1, "Pt", [128, 2, 1024], BF16)
    accT = SB(es1, "accT", [128, 4, 512], F32)
    mixed = SB(es1, "mixed", [128, 4, 512], BF16)
    mixedT = SB(es1, "mixedT", [128, 4, 512], BF16)
    x1stg = SB(es1, "x1stg", [128, D], F32)
    pst1 = SB(es1, "pst1", [128, 2, 4], F32)
    pntmp1 = SB(es1, "pntmp1", [128, 512], F32)
    rden = SB(es1, "rden", [128, 2, 4], F32)
    gbc1 = SB(es1, "gbc1", [128, D], F32)
    if _os.environ.get("KDEBUG"):
        print("sweep1 sbuf bytes remaining:", nc.sbuf_bytes_remaining)
    DMA("sp", gbc1[:], bc_d[0].partition_broadcast(128), [], ["gbc1"], "c0_6")
    for t_ in (kT[1], kT[2], kT[3], V1c, V2c, V3c, Vst):
        P.emit("pool", lambda e, t_=t_: e.memset(t_[:], 0.0), [], [("cache", id(t_))])
    CACHE_INIT = [("cache", id(t_)) for t_ in (kT[1], kT[2], kT[3], V1c, V2c, V3c, Vst)]
    P.emit("pool", lambda e: e.memset(accT[:], 0.0), [], ["accT"])

    bank_state.update(n=0, ring=[0, 1, 2, 3])
    sb_state = dict(s=0, o=0, p=0, x=0, q=0)

    def sbank():
        k = 4 + sb_state["s"] % 2
        sb_state["s"] += 1
        return k

    def obank():
        k = 6 + sb_state["o"] % 2
        sb_state["o"] += 1
        return k

    def kbase(g, t, b):
        if g == 1:
            return ((4 * t + b) % 5) * 128
        if g == 2:
            return (t % 2) * 512 + b * 128
        return ((t // 4) % 2) * 2048 + (t % 4) * 512 + b * 128

    stage1_done = set()

    def stage1(t):
        stage1_done.add(t)
        for b in range(4):
            B = 4 * t + b
            xslot = sb_state["x"] % 2
            sb_state["x"] += 1
            DMA("sp", xsl[:, xslot, :], xs_d[B * 128:(B + 1) * 128, :], [], [("xsl", xslot)], ("xsl", xslot))
            norm_to_T(xsl[:, xslot, :], ("xsl", xslot), PP_GT + 0, hn1, junk1[:], nst1, xslot,
                      hnT[:, :, b * 128:(b + 1) * 128], ("hnT", b))

    _s1_tiles = int(_os.environ.get("S1_TILES", NTILE))
    _s1_stage = int(_os.environ.get("S1_STAGE", 9))
    for t in range(NTILE):
        if t >= _s1_tiles:
            break
        chunks = s1_tile_chunks(t)
        prepass_some(6)
        if t == 11:
            while _pre_list:
                prepass_some(6)
            prepass_fix(("pre3", "pre4"))
        kv_groups = s1_kv_groups(t)
        qblocks = [] if t <= 3 else ([3] if t == 4 else [0, 1, 2, 3])
        if t not in stage1_done:
            stage1(t)
        if _s1_stage < 2:
            continue
        bank_state["ring"] = list(range(8))
        pend_b = [None]
        for nm in chunks:
            if nm == "wo":
                continue
            Wap, Wres = wnext(nm)
            W3 = Wap.rearrange("p (kc n) -> p kc n", kc=8)
            kind, g = nm[0], int(nm[1])
            if kind == "v" and _os.environ.get("S1_NOV"):
                continue
            if kind != "v" and _os.environ.get("S1_NOK"):
                continue
            blks = list(qblocks if kind == "q" else range(4))
            if kind == "v":
                for b in blks:
                    B = 4 * t + b
                    k = mmbank()
                    for kc in range(8):
                        MM(ps[k][:], hnT[:, kc, b * 128:(b + 1) * 128], W3[:, kc, :], kc == 0, kc == 7,
                           [("hnT", b), Wres], [("ps", k)])
                    pv = ps[k][:].rearrange("p (h d) -> p h d", d=64)
                    if g == 1:
                        dst = V1c[:, B % 5, :, :]
                        dres = ("v1", B % 5)
                    else:
                        dst = Vst[:, b, :, :]
                        dres = ("vst", b)
                    ACT(dst[:, :, 0:64], pv, AF.Copy, [("ps", k)] + CACHE_INIT, [dres])
                    CP("pool", dst[:, :, 64:65], tm[:, B:B + 1].unsqueeze(1).to_broadcast([128, 8, 1]), ["tm"], [dres])
            else:
                sl = sb_state["q"] % 2
                sb_state["q"] += 1
                b0, nbk = blks[0], len(blks)
                for b in blks:
                    k = mmbank()
                    for kc in range(8):
                        MM(ps[k][:], hnT[:, kc, b * 128:(b + 1) * 128], W3[:, kc, :], kc == 0, kc == 7,
                           [("hnT", b), Wres], [("ps", k)])
                    ACT(stg[:, sl, b, :], ps[k][:], AF.Copy, [("ps", k)], [("stg", sl, b)])
                sv = stg[:, sl, b0:b0 + nbk, :].rearrange("p b (h d) -> p b h d", d=64)
                B0 = 4 * t + b0
                cb = cosT[:, B0:B0 + nbk, :].unsqueeze(2).to_broadcast([128, nbk, 8, 8])
                sn = sinT[:, B0:B0 + nbk, :].unsqueeze(2).to_broadcast([128, nbk, 8, 8])
                tmp = [rtmp[:, i, 0:nbk * 64].rearrange("p (b h d) -> p b h d", b=nbk, h=8) for i in range(4)]
                t1, t2 = sv[:, :, :, 0:8], sv[:, :, :, 8:16]
                sres = [("stg", sl, b) for b in blks]
                TT("dve", tmp[0], t1, cb, ALU.mult, sres + ROPE_RES, ["rt0"])
                TT("dve", tmp[1], t2, sn, ALU.mult, sres + ROPE_RES, ["rt1"])
                TT("dve", tmp[2], t2, cb, ALU.mult, sres + ROPE_RES, ["rt2"])
                TT("dve", tmp[3], t1, sn, ALU.mult, sres + ROPE_RES, ["rt3"])
                TT("dve", t1, tmp[0], tmp[1], ALU.subtract, ["rt0", "rt1"], sres)
                TT("dve", t2, tmp[2], tmp[3], ALU.add, ["rt2", "rt3"], sres)
                def part_b(blks=blks, sl=sl, kind=kind, g=g, t=t):
                    for b in blks:
                        k2 = mmbank()
                        for c in range(4):
                            TR(psb[k2][:, c * 128:(c + 1) * 128], stg[:, sl, b, c * 128:(c + 1) * 128], identB[:],
                               [("stg", sl, b), "identB"], [("ps", k2)])
                        src = psb[k2][:, 0:512].rearrange("p (c n) -> p c n", c=4)
                        if kind == "q":
                            CP("act", qT[:, (g - 1) * 4:g * 4, b * 128:(b + 1) * 128], src, [("ps", k2)], [("qT", g, b)])
                        else:
                            o = kbase(g, t, b)
                            kres = ("k", g, (4 * t + b) % 5) if g == 1 else (("k", g, t % 2) if g == 2 else ("k", g, (t // 4) % 2))
                            CP("act", kT[g][:, :, o:o + 128], src, [("ps", k2)] + CACHE_INIT, [kres])

                if pend_b[0] is not None:
                    pend_b[0]()
                pend_b[0] = part_b
            if kind == "v" and g in (2, 3) and not _os.environ.get("S1_NOVDMA"):
                rows = Vd[g][t * 512:(t + 1) * 512, :]
                DMA("sp", rows.rearrange("(b p) f -> p b f", p=128), Vst[:].rearrange("p b h d -> p b (h d)"),
                    [("vst", b) for b in range(4)], [("vd", g)], ("vd", g))
                if g == 2:
                    DMA("sp", V2c[:, t % 2, :, :], rows.rearrange("(a r) f -> a r f", r=4),
                        [("vd", g)] + CACHE_INIT, [("v2", t % 2)], ("v2c", t % 2))
                else:
                    j = t % 4
                    DMA("sp", V3c[32 * j:32 * j + 32, (t // 4) % 2, :, :], rows.rearrange("(a r) f -> a r f", r=16),
                        [("vd", g)] + CACHE_INIT, [("v3", (t // 4) % 2)], ("v3c", (t // 4) % 2))
        if pend_b[0] is not None:
            pend_b[0]()
            pend_b[0] = None
        if not qblocks or _s1_stage < 3:
            if t + 1 < min(NTILE, _s1_tiles):
                stage1(t + 1)
            continue
        bank_state["ring"] = [0, 1]
        Wo_ap, Wo_res = wnext("wo")
        Wo3 = Wo_ap.rearrange("p (kc n) -> p kc n", kc=4)
        q_lo = qblocks[0] * 128
        units = []
        for b in qblocks:
            B = 4 * t + b
            units.append(dict(g=1, a0=0, nq=128, qs=[b * 128], qd=1, nph=1,
                              kp=[((B - 1) % 5) * 128], ko=[(B % 5) * 128], kd=1,
                              vp=lambda h, f, B=B: V1c[:, (B - 1) % 5, h, :], vo=lambda h, f, B=B: V1c[:, B % 5, h, :],
                              rk=[("k", 1, (B - 1) % 5), ("k", 1, B % 5)], rv=[("v1", (B - 1) % 5), ("v1", B % 5)],
                              rq=[("qT", 1, b)]))
        for r in range(4):
            a0 = q_lo // 4
            nq = (512 - q_lo) // 4
            units.append(dict(g=2, a0=a0, nq=nq, qs=[q_lo + r], qd=4, nph=1,
                              kp=[((t - 1) % 2) * 512 + r], ko=[(t % 2) * 512 + r], kd=4,
                              vp=lambda h, f, r=r: V2c[:, (t - 1) % 2, r, h * 65:(h + 1) * 65],
                              vo=lambda h, f, r=r: V2c[:, t % 2, r, h * 65:(h + 1) * 65],
                              rk=[("k", 2, 0), ("k", 2, 1)], rv=[("v2", 0), ("v2", 1)],
                              rq=[("qT", 2, b) for b in qblocks]))
        for r0 in range(0, 16, 4):
            sbp = (t // 4) % 2
            a0 = (t % 4) * 32 + q_lo // 16
            nq = (512 - q_lo) // 16
            units.append(dict(g=3, a0=a0, nq=nq, qd=16, kd=16, nph=4, r0=r0,
                              qs=[q_lo + r0 + f for f in range(4)],
                              kp=[(1 - sbp) * 2048 + r0 + f for f in range(4)], ko=[sbp * 2048 + r0 + f for f in range(4)],
                              vp=lambda h, f, r0=r0, sbp=sbp: V3c[:, 1 - sbp, r0 + f, h * 65:(h + 1) * 65],
                              vo=lambda h, f, r0=r0, sbp=sbp: V3c[:, sbp, r0 + f, h * 65:(h + 1) * 65],
                              rk=[("k", 3, 0), ("k", 3, 1)], rv=[("v3", 0), ("v3", 1)],
                              rq=[("qT", 3, b) for b in qblocks]))
        if _os.environ.get("S1_G"):
            units = [u for u in units if str(u["g"]) in _os.environ["S1_G"]]
        _att = int(_os.environ.get("S1_ATT", 9))

        def stage_a(hq, u):
            g, nq, a0, nph = u["g"], u["nq"], u["a0"], u["nph"]
            sp_ = sb_state["s"] % 2
            sb_state["s"] += 1
            sbk = [4 + 2 * sp_, 5 + 2 * sp_]
            ncol = 4 * nph * nq
            Sv = [ps[sbk[eo]][:, 0:ncol].rearrange("p (i f u n) -> p i f u n", i=2, f=nph, u=2) for eo in range(2)]
            for hh in range(4):
                h = hq * 4 + hh
                c = (g - 1) * 4 + h // 2
                pr = slice(64 * (h % 2), 64 * (h % 2) + 64)
                for f in range(nph):
                    qs = u["qs"][f]
                    qap = qT[pr, c, qs:qs + u["qd"] * (nq - 1) + 1:u["qd"]]
                    for ui2, off in enumerate((u["kp"][f], u["ko"][f])):
                        kap = kT[g][pr, h // 2, off:off + u["kd"] * 127 + 1:u["kd"]]
                        MM(Sv[h % 2][:, hh // 2, f, ui2, :], kap, qap, True, True, u["rk"] + u["rq"], [("ps", sbk[h % 2])])
            psl = sb_state["p"] % 2
            sb_state["p"] += 1
            for eo in range(2):
                ACT(Pt[:, psl, eo * ncol:(eo + 1) * ncol], ps[sbk[eo]][:, 0:ncol], AF.Exp, [("ps", sbk[eo])], [("Pt", psl, eo)], scale=0.125)
            Pm = Pt[:, psl, 0:2 * ncol].rearrange("p (j u n) -> p j u n", j=4 * nph, u=2)
            mk = maskUL[:, :, a0:a0 + nq].unsqueeze(1).to_broadcast([128, 4 * nph, 2, nq])
            pres = [("Pt", psl, 0), ("Pt", psl, 1)]
            P.emit("dve", lambda e, Pm=Pm, mk=mk: e.tensor_tensor(out=Pm, in0=Pm, in1=mk, op=ALU.mult), pres + ["maskUL"], pres)
            Pv = Pt[:, psl, 0:2 * ncol].rearrange("p (e i f u n) -> p e i f u n", e=2, i=2, f=nph, u=2)
            return Pv, pres

        def stage_b(hq, u, Pv, pres):
            g, nq, nph = u["g"], u["nq"], u["nph"]
            ob = 2 + sb_state["o"] % 2
            sb_state["o"] += 1
            Ov = ps[ob][:, 0:4 * nph * nq].rearrange("p (h f n) -> p h f n", h=4, f=nph)
            for hh in range(4):
                h = hq * 4 + hh
                for f in range(nph):
                    MM(Ov[0:65, hh, f, :], u["vp"](h, f), Pv[:, h % 2, hh // 2, f, 0, :], True, False, pres + u["rv"], [("ps", ob)])
                    MM(Ov[0:65, hh, f, :], u["vo"](h, f), Pv[:, h % 2, hh // 2, f, 1, :], False, True, pres + u["rv"], [("ps", ob)])
            if nph == 1:
                qs = u["qs"][0]
                dst = accT[0:65, :, qs:qs + u["qd"] * (nq - 1) + 1:u["qd"]]
                src = Ov[0:65, :, 0, :]
            else:
                dst = accT[0:65, :, q_lo:q_lo + 16 * nq].rearrange("p h (a r) -> p h r a", r=16)[:, :, u["r0"]:u["r0"] + nph, :]
                src = Ov[0:65, :, :, :]
            if g == 1:
                CP("act", dst, src, [("ps", ob)], ["accT"])
            else:
                TT("dve", dst, dst, src, ALU.add, [("ps", ob)], ["accT"])

        def normalise(hq):
            for b in (qblocks if _s1_stage >= 4 else []):
                k = mmbank()
                Tv = ps[k][:, 0:260].rearrange("p (h d) -> p h d", h=4)
                for hh in range(4):
                    TR(Tv[:, hh, :], accT[0:65, hh, b * 128:(b + 1) * 128], identF[0:65, 0:65], ["accT", "identF"], [("ps", k)])
                rs_ = ("rden", b % 2)
                TS("dve", rden[:, b % 2, :], Tv[:, :, 64], 1e-30, None, ALU.max, None, [("ps", k)], [rs_])
                P.emit("dve", lambda e, b=b: e.reciprocal(out=rden[:, b % 2, :], in_=rden[:, b % 2, :]), [rs_], [rs_])
                mres = ("mixed", b)
                mv = mixed[:, b, hq * 256:(hq + 1) * 256].rearrange("p (h d) -> p h d", h=4)
                TT("dve", mv, Tv[:, :, 0:64], rden[:, b % 2, :].unsqueeze(2).to_broadcast([128, 4, 64]), ALU.mult,
                   [("ps", k), rs_], [mres])
                if hq == 1:
                    k2 = mmbank()
                    for c in range(4):
                        TR(psb[k2][:, c * 128:(c + 1) * 128], mixed[:, b, c * 128:(c + 1) * 128], identB[:],
                           [mres, "identB"], [("ps", k2)])
                    CP("act", mixedT[:, :, b * 128:(b + 1) * 128], psb[k2][:, 0:512].rearrange("p (c n) -> p c n", c=4),
                       [("ps", k2)], [("mixedT", b)])

        for hq in range(2):
            pend = None
            for u in units + [None]:
                cur = None
                if u is not None:
                    cur = (u,) + stage_a(hq, u)
                if pend is not None:
                    stage_b(hq, pend[0], pend[1], pend[2])
                pend = cur
            if hq == 0 and t + 1 < min(NTILE, _s1_tiles):
                stage1(t + 1)
            if hq == 1:
                bank_state["ring"] = [0, 1, 4, 5, 6, 7]
            normalise(hq)
        for b in (qblocks if _s1_stage >= 5 else []):
            B = 4 * t + b
            yb = []
            for hf in range(2):
                k = mmbank()
                for kc in range(4):
                    MM(ps[k][:], mixedT[:, kc, b * 128:(b + 1) * 128], Wo3[:, kc, hf * 512:(hf + 1) * 512], kc == 0, kc == 3,
                       [("mixedT", b), Wo_res], [("ps", k)])
                yb.append(k)
            DMA("sp", x1stg[:], xs_d[B * 128:(B + 1) * 128, :], [], ["x1stg"], "x1in")
            postnorm_resid([ps[yb[0]][:], ps[yb[1]][:]], [("ps", yb[0]), ("ps", yb[1])], gbc1, "gbc1",
                           x1stg, "x1stg", pst1, b % 2, junk1, pntmp1)
            DMA("sp", x1_d[(B - HALO_BLK) * 128:(B - HALO_BLK + 1) * 128, :], x1stg[:], ["x1stg"], [("x1d", B)], "x1out")

    while _pre_list:
        prepass_some(6)
    prepass_fix(("pre3", "pre4"))
    P.barrier()
    es1.close()
    if phase == 1:
        if "x1out" in P.dma_count:
            P.ins["sp"].append(dict(fn=None, waits=[("d", "x1out", P.dma_count["x1out"])], dma=None))
        with ExitStack() as esm:
            esem = {e: esm.enter_context(nc.semaphore("sem_" + e)) for e in Prog.ENG}
            dsem = {}
            for i, k in enumerate(sorted(P.dma_count.keys(), key=str)):
                dsem[k] = esm.enter_context(nc.semaphore(f"dsem{i}"))
            block = esm.enter_context(nc.Block())
            P.replay(nc, block, esem, dsem)
        es0.close()
        return nc

    es2 = ExitStack()
    XR = SB(es2, "XR", [128, 2, 4, D], F32)
    junk2 = SB(es2, "junk2", [128, D], BF16)
    hn2 = SB(es2, "hn2", [128, 4, D], BF16)
    nst2 = SB(es2, "nst2", [128, 4, 4], F32)
    hnT2 = SB(es2, "hnT2", [128, 8, 512], BF16)
    hT = SB(es2, "hT", [128, 22, 512], BF16)
    zsb = SB(es2, "zsb", [128, 2, 514], BF16)
    dgf = SB(es2, "dgf", [128, 2, 3, 128], BF16)
    cacc = SB(es2, "cacc", [128, 2, 512], F32)
    corr = SB(es2, "corr", [128, 2, 2, 22, 2], F32)
    dg31 = SB(es2, "dg31", [128, 2, 31, 128], BF16)
    sgt = SB(es2, "sgt", [128, 2, 512], F32)
    carry = SB(es2, "carry", [128, 2, 44, 2], BF16)
    uT = SB(es2, "uT", [128, 8, 30 + 512], BF16)
    cT = SB(es2, "cT", [128, 8, 512], F32)
    cb16 = SB(es2, "cb16", [128, 8, 512], BF16)
    lnm = SB(es2, "lnm", [128, 3, 512], F32)
    sT = SB(es2, "sT", [128, 8, 512], BF16)
    ones_row = SB(es2, "ones_row", [1, 128], BF16)
    bias_row = SB(es2, "bias_row", [1, D], BF16)
    pst2 = SB(es2, "pst2", [128, 4, 4], F32)
    pntmp2 = SB(es2, "pntmp2", [128, 512], F32)
    gbc = SB(es2, "gbc", [128, 4, D], F32)
    if _os.environ.get("KDEBUG"):
        print("sweep2 sbuf bytes remaining:", nc.sbuf_bytes_remaining)
    for i in range(4):
        DMA("sp", gbc[:, i, :], bc_d[i + 1].partition_broadcast(128), [], [("gbc", i)], ("c1", i))
    P.emit("pool", lambda e: e.memset(carry[:], 0.0), [], ["carry"])
    P.emit("pool", lambda e: e.memset(corr[:], 0.0), [], ["corr"])
    P.emit("pool", lambda e: e.memset(ones_row[:], 1.0), [], ["ones_row"])
    P.emit("pool", lambda e: e.dma_start(out=bias_row[:], in_=bc_d[4:5, :]), [], ["bias_row"], dma="c1_bias")
    P.emit("pool", lambda e: e.memset(uT[:], 0.0), [], ["uT"])
    bank_state.update(n=0, ring=list(range(8)))

    ffn_state = dict(par=0)

    def ffn_norm(L, nb, xr, xr_res_f, halo):
        for b in range(nb):
            norm_to_T(xr[:, b, :], xr_res_f(b), PP_GT + 8 * (1 + 2 * L), hn2, junk2[:], nst2, b,
                      hnT2[:, :, b * 128:(b + 1) * 128], ("hnT2", b),
                      extra_scale=(flag[:, 0:1] if (halo and L == 1) else None))

    def ffn(L, nt, nb, xr, xr_res_f, gidx, halo, skip_norm=False, pre_down_hook=None, next_norm=None):
        if not skip_norm:
            ffn_norm(L, nb, xr, xr_res_f, halo)
        hres = [("hnT2", b) for b in range(nb)]
        pend = None
        Wcur = [None, None]
        par = ffn_state["par"]
        for j in range(23):
            kb = None
            if j < 22:
                i, pl = j // 2, j % 2
                if pl == 0:
                    Wap, Wres = wnext(("up", L, i))
                    Wcur = [Wap.rearrange("p (kc u n) -> p kc u n", kc=8, u=2), Wres]
                W4, Wres = Wcur
                kb = []
                for ug in range(2):
                    k = mmbank()
                    for kc in range(8):
                        MM(ps[k][:, 0:nt], W4[:, kc, ug, pl * 128:(pl + 1) * 128], hnT2[:, kc, 0:nt], kc == 0, kc == 7,
                           hres + [Wres], [("ps", k)])
                    kb.append(k)
                ch = j
                sl = j % 2
                wc = PP_FW + (L * 44 + ch) * 3
                ACT(zsb[:, sl, 2:2 + nt], ps[kb[0]][:, 0:nt], AF.Copy, [("ps", kb[0])], [("zsb", sl)])
                cres = ("carry", L, ch)
                CP("pool", zsb[:, sl, 0:2], carry[:, L, ch, :], ["carry", cres], [("zsbc", sl)])
                CP("pool", carry[:, L, ch, :], zsb[:, sl, nt:nt + 2], [("zsb", sl)], [cres])
                for tap in range(3):
                    ACT(dgf[:, sl, tap, :], identB[:], AF.Identity, ["identB", "pp"], [("dgf", sl)], scale=pp[:, wc + tap:wc + tap + 1])
                ch = 22 + j
                wc = PP_FW + (L * 44 + ch) * 3
                bcol = PP_FB + L * 44 + ch
                z = ps[kb[1]]
                ares = ("cacc", sl)
                zr = [("ps", kb[1])]
                TS("dve", cacc[:, sl, 0:nt], z[:, 0:nt], pp[:, wc + 2:wc + 3], pp[:, bcol:bcol + 1], ALU.mult, ALU.add, zr + ["pp"], [ares])
                STT(cacc[:, sl, 1:nt], z[:, 0:nt - 1], pp[:, wc + 1:wc + 2], cacc[:, sl, 1:nt], ALU.mult, ALU.add, zr, [ares])
                STT(cacc[:, sl, 2:nt], z[:, 0:nt - 2], pp[:, wc:wc + 1], cacc[:, sl, 2:nt], ALU.mult, ALU.add, zr, [ares])
                TT("dve", cacc[:, sl, 0:2], cacc[:, sl, 0:2], corr[:, par, L, j, :], ALU.add, ["corr", ("corr", par, L, j)], [ares])
                nres = ("corr", 1 - par, L, j)
                TS("dve", corr[:, 1 - par, L, j, :], z[:, nt - 2:nt], pp[:, wc:wc + 1], None, ALU.mult, None, zr + ["corr"], [nres])
                STT(corr[:, 1 - par, L, j, 0:1], z[:, nt - 1:nt], pp[:, wc + 1:wc + 2], corr[:, 1 - par, L, j, 0:1], ALU.mult, ALU.add,
                    zr, [nres])
                ACT(sgt[:, sl, 0:nt], cacc[:, sl, 0:nt], AF.Silu, [ares], [("sgt", sl)])
            if pend is not None:
                pj, pkb = pend
                sl = pj % 2
                k = mmbank()
                for tap in range(3):
                    MM(ps[k][:, 0:nt], dgf[:, sl, tap, :], zsb[:, sl, tap:tap + nt], tap == 0, tap == 2,
                       [("zsb", sl), ("zsbc", sl), ("dgf", sl)], [("ps", k)])
                bu = PP_FB + L * 44 + pj
                STT(hT[:, pj, 0:nt], ps[k][:, 0:nt], pp[:, bu:bu + 1], sgt[:, sl, 0:nt], ALU.add, ALU.mult,
                    [("ps", k), ("sgt", sl), "pp"], [("hT", pj)])
            pend = (j, kb) if j < 22 else None
        if L == 1:
            ffn_state["par"] = 1 - par
        if pre_down_hook is not None:
            pre_down_hook()
        ybank = {}
        for hf in range(2):
            for b in range(nb):
                ybank[(b, hf)] = mmbank()
            for pc in range(3):
                Wap, Wres = wnext(("dn", L, hf, pc))
                W3 = Wap.rearrange("p (kc n) -> p kc n", kc=8)
                k0, k1 = [(0, 8), (8, 16), (16, 22)][pc]
                for b in range(nb):
                    k = ybank[(b, hf)]
                    for kc in range(k0, k1):
                        MM(ps[k][:], hT[:, kc, b * 128:(b + 1) * 128], W3[:, kc - k0, :], kc == 0, kc == 21,
                           [("hT", kc), Wres], [("ps", k)])
        if next_norm is not None:
            postnorm_then_norm(nb, [(ybank[(b, 0)], ybank[(b, 1)]) for b in range(nb)], gbc[:, gidx, :], ("gbc", gidx),
                               xr, xr_res_f, next_norm[0], next_norm[1])
        else:
            for b in range(nb):
                k0_, k1_ = ybank[(b, 0)], ybank[(b, 1)]
                postnorm_resid([ps[k0_][:], ps[k1_][:]], [("ps", k0_), ("ps", k1_)], gbc[:, gidx, :], ("gbc", gidx),
                               xr[:, b, :], xr_res_f(b), pst2, b, junk2, pntmp2)

    def conformer(nt, nb, xr, xr_res_f, halo, nt_prev, skip_norm=False, next_norm=None):
        for b in (range(nb) if not skip_norm else []):
            norm_to_T(xr[:, b, :], xr_res_f(b), PP_GT + 16, hn2, junk2[:], nst2, b,
                      hnT2[:, :, b * 128:(b + 1) * 128], ("hnT2", b))
        hres = [("hnT2", b) for b in range(nb)]
        if nt_prev is not None:
            CP("pool", uT[:, :, 0:30], uT[:, :, nt_prev:nt_prev + 30], ["uT"] + [("uT", c) for c in range(8)], ["uTc"])
        for i in range(4):
            Wap, Wres = wnext(("pw1", i))
            W4 = Wap.rearrange("p (kc u n) -> p kc u n", kc=8, u=2)
            for pl in range(2):
                c = 2 * i + pl
                kb = []
                for ug in range(2):
                    k = mmbank()
                    for kc in range(8):
                        MM(ps[k][:, 0:nt], W4[:, kc, ug, pl * 128:(pl + 1) * 128], hnT2[:, kc, 0:nt], kc == 0, kc == 7,
                           hres + [Wres], [("ps", k)])
                    kb.append(k)
                ACT(sgt[:, c % 2, 0:nt], ps[kb[1]][:, 0:nt], AF.Sigmoid, [("ps", kb[1]), "pp"], [("sgt", c % 2)],
                    bias=pp[:, PP_BPW1 + 8 + c:PP_BPW1 + 9 + c])
                STT(uT[:, c, 30:30 + nt], ps[kb[0]][:, 0:nt], pp[:, PP_BPW1 + c:PP_BPW1 + c + 1], sgt[:, c % 2, 0:nt],
                    ALU.add, ALU.mult, [("ps", kb[0]), ("sgt", c % 2), "uTc", "uT"], [("uT", c)])
                if halo:
                    TS("dve", uT[:, c, 30:30 + nt], uT[:, c, 30:30 + nt], flag[:, 0:1], None, ALU.mult, None, ["flag"], [("uT", c)])
        for c in range(8):
            wc = PP_WDW + c * 31
            for j in range(31):
                if j % 2 == 0:
                    TS("dve", dg31[:, c % 2, j, :], identB[:], pp[:, wc + j:wc + j + 1], None, ALU.mult, None,
                       ["identB", "pp"], [("dg31", c % 2, j)])
                else:
                    ACT(dg31[:, c % 2, j, :], identB[:], AF.Identity, ["identB", "pp"], [("dg31", c % 2, j)],
                        scale=pp[:, wc + j:wc + j + 1])
            k = mmbank()
            for j in range(31):
                MM(ps[k][:, 0:nt], dg31[:, c % 2, j, :], uT[:, c, j:j + nt], j == 0, j == 30,
                   [("uT", c), "uTc", ("dg31", c % 2, j)], [("ps", k)])
            ACT(cT[:, c, 0:nt], ps[k][:, 0:nt], AF.Identity, [("ps", k), "pp"], [("cT", c)], bias=pp[:, PP_BDW + c:PP_BDW + c + 1])
        km, kq = mmbank(), mmbank()
        for c in range(8):
            ACT(cb16[:, c, 0:nt], cT[:, c, 0:nt], AF.Copy, [("cT", c)], [("cb16", c)])
        for c in range(8):
            MM(ps[km][:, 0:nt], onesdiv[:], cb16[:, c, 0:nt], c == 0, c == 7, [("cb16", c), "onesdiv"], [("ps", km)])
        for c in range(8):
            ACT(cb16[:, c, 0:nt], cT[:, c, 0:nt], AF.Square, [("cT", c)], [("cb16", c)])
        for c in range(8):
            MM(ps[kq][:, 0:nt], onesdiv[:], cb16[:, c, 0:nt], c == 0, c == 7, [("cb16", c), "onesdiv"], [("ps", kq)])
        mean, var, rstd = lnm[:, 0, 0:nt], lnm[:, 1, 0:nt], lnm[:, 2, 0:nt]
        CP("dve", mean, ps[km][:, 0:nt], [("ps", km)], ["ln_mean"])
        TT("dve", var, mean, mean, ALU.mult, ["ln_mean"], ["ln_var"])
        TT("dve", var, ps[kq][:, 0:nt], var, ALU.subtract, [("ps", kq)], ["ln_var"])
        TS("dve", var, var, 0.0, None, ALU.max, None, [], ["ln_var"])
        ACT(rstd, var, AF.Sqrt, ["ln_var"], ["ln_rstd"], bias=EPS)
        P.emit("dve", lambda e: e.reciprocal(out=rstd, in_=rstd), ["ln_rstd"], ["ln_rstd"])
        for c in range(8):
            TT("dve", cT[:, c, 0:nt], cT[:, c, 0:nt], mean, ALU.subtract, ["ln_mean"], [("cT", c)])
            TT("dve", cT[:, c, 0:nt], cT[:, c, 0:nt], rstd, ALU.mult, ["ln_rstd"], [("cT", c)])
            ACT(sT[:, c, 0:nt], cT[:, c, 0:nt], AF.Silu, [("cT", c), "pp"], [("sT", c)],
                scale=pp[:, PP_LNG + c:PP_LNG + c + 1], bias=pp[:, PP_LNB + c:PP_LNB + c + 1])
        ybank = {}
        for hf in range(2):
            Wap, Wres = wnext(("pw2", hf))
            W3 = Wap.rearrange("p (kc n) -> p kc n", kc=8)
            for b in range(nb):
                k = mmbank()
                ybank[(b, hf)] = k
                for kc in range(8):
                    MM(ps[k][:], sT[:, kc, b * 128:(b + 1) * 128], W3[:, kc, :], kc == 0, False, [("sT", kc), Wres], [("ps", k)])
                MM(ps[k][:], ones_row[:], bias_row[:, hf * 512:(hf + 1) * 512], False, True, ["ones_row", "bias_row"], [("ps", k)])
        if next_norm is not None:
            postnorm_then_norm(nb, [(ybank[(b, 0)], ybank[(b, 1)]) for b in range(nb)], gbc[:, 1, :], ("gbc", 1),
                               xr, xr_res_f, next_norm[0], next_norm[1])
        else:
            for b in range(nb):
                k0_, k1_ = ybank[(b, 0)], ybank[(b, 1)]
                postnorm_resid([ps[k0_][:], ps[k1_][:]], [("ps", k0_), ("ps", k1_)], gbc[:, 1, :], ("gbc", 1),
                               xr[:, b, :], xr_res_f(b), pst2, b, junk2, pntmp2)

    tiles2 = [dict(B0=HALO_BLK, nb=1, halo=True)] + [dict(B0=20 + 4 * j, nb=4, halo=False) for j in range(8)]

    def tile_ctx(ti):
        tl = tiles2[ti]
        xs_ = ti % 2
        return tl, XR[:, xs_, :, :], (lambda b, xs_=xs_: ("XR", xs_, b)), xs_

    def load_and_norm(ti):
        tl, xr, xr_res_f, xs_ = tile_ctx(ti)
        for b in range(tl["nb"]):
            DMA("sp", xr[:, b, :], x1_d[(tl["B0"] + b - HALO_BLK) * 128:(tl["B0"] + b - HALO_BLK + 1) * 128, :],
                [("x1d", tl["B0"] + b)], [xr_res_f(b)], ("xr", xs_, b))
        ffn_norm(0, tl["nb"], xr, xr_res_f, tl["halo"])

    nt_prev = None
    load_and_norm(0)
    for ti in range(len(tiles2)):
        tl, xr, xr_res_f, xs_ = tile_ctx(ti)
        nb, B0 = tl["nb"], tl["B0"]
        nt = nb * 128
        ffn(0, nt, nb, xr, xr_res_f, 0, tl["halo"], skip_norm=True, next_norm=(PP_GT + 16, None))
        conformer(nt, nb, xr, xr_res_f, tl["halo"], nt_prev, skip_norm=True,
                  next_norm=(PP_GT + 24, flag[:, 0:1] if tl["halo"] else None))
        hook = (lambda ti=ti: load_and_norm(ti + 1)) if ti + 1 < len(tiles2) else None
        ffn(1, nt, nb, xr, xr_res_f, 2, tl["halo"], skip_norm=True, pre_down_hook=hook)
        nt_prev = nt
        if not tl["halo"]:
            for b in range(nb):
                r0 = (B0 - 20 + b) * 128
                DMA("sp", out_d[r0:r0 + 128, :], xr[:, b, :], [xr_res_f(b)], [("outd", r0)], ("out", xs_, b))
    P.ins["sp"].append(dict(fn=None, waits=[("d", k, v) for k, v in P.dma_count.items()
                                            if (isinstance(k, tuple) and k[0] == "out") or k == "x1out"], dma=None))
    assert wstate["used"] == len(wseq)

    with ExitStack() as esm:
        esem = {e: esm.enter_context(nc.semaphore("sem_" + e)) for e in Prog.ENG}
        dsem = {}
        for i, k in enumerate(sorted(P.dma_count.keys(), key=str)):
            dsem[k] = esm.enter_context(nc.semaphore(f"dsem{i}"))
        block = esm.enter_context(nc.Block())
        P.replay(nc, block, esem, dsem)
    es2.close()
    es0.close()
    return nc


def _host_prep(inputs):
    f = np.float32
    x = np.asarray(inputs["x"], f)
    pos = np.asarray(inputs["positions"]).astype(np.int32)
    g = np.asarray(inputs["norm_g"], f)
    pp = np.zeros((128, PP_N), f)

    def fm(v):
        return np.asarray(v, f).reshape(-1, 128).T

    for i, (L, j) in enumerate([(0, 0), (0, 2), (1, 0), (1, 2)]):
        pp[:, PP_GT + 8 * i:PP_GT + 8 * i + 8] = fm(g[L, j])
    pp[:, PP_BPW1:PP_BPW1 + 16] = fm(inputs["conv_b_pw1"][0])
    wdw = np.asarray(inputs["conv_w_dw"][0], f)
    pp[:, PP_WDW:PP_WDW + 248] = wdw.reshape(31, 8, 128).transpose(2, 1, 0).reshape(128, 248)
    pp[:, PP_BDW:PP_BDW + 8] = fm(inputs["conv_b_dw"][0])
    pp[:, PP_LNG:PP_LNG + 8] = fm(inputs["conv_ln_g"][0])
    pp[:, PP_LNB:PP_LNB + 8] = fm(inputs["conv_ln_b"][0])
    fw = np.asarray(inputs["ffn_w_dw"], f)
    pp[:, PP_FW:PP_FW + 264] = fw.reshape(2, 3, 44, 128).transpose(3, 0, 2, 1).reshape(128, 264)
    fb = np.asarray(inputs["ffn_b_dw"], f)
    pp[:, PP_FB:PP_FB + 88] = fb.reshape(2, 44, 128).transpose(2, 0, 1).reshape(128, 88)
    bc = np.stack([g[0, 1], g[0, 3], g[1, 1], g[1, 3], np.asarray(inputs["conv_b_pw2"][0], f)]).astype(f)
    cst = np.zeros((128, C_N), f)
    cst[:, C_ID:C_ID + 128] = np.eye(128, dtype=f)
    cidx = np.arange(128)[:, None]
    aidx = np.arange(128)[None, :]
    cst[:, C_MASK:C_MASK + 128] = (cidx >= aidx).astype(f)
    cst[:, C_MASK + 128:C_MASK + 256] = (cidx <= aidx).astype(f)
    half = 8
    invf = (500000.0 ** (-np.arange(half, dtype=np.float32) / half)).astype(f)
    cst[:, C_INVF:C_INVF + 8] = invf[None, :]
    shared = dict(
        w_qkv=np.ascontiguousarray(inputs["attn_w_qkv"][0], dtype=f),
        w_o=np.ascontiguousarray(inputs["attn_w_o"][0], dtype=f),
        w_pw1=np.ascontiguousarray(inputs["conv_w_pw1"][0], dtype=f),
        w_pw2=np.ascontiguousarray(inputs["conv_w_pw2"][0], dtype=f),
        w_up=np.ascontiguousarray(inputs["ffn_w_up"], dtype=f),
        w_down=np.ascontiguousarray(inputs["ffn_w_down"], dtype=f),
        pp=pp, bc=bc, cst=cst)
    in_maps = []
    for c in range(NCORES):
        b, hf = c // 2, c % 2
        s0 = hf * 4096
        lo = s0 - HALO
        xs = np.zeros((NLOC, D), f)
        ps_ = np.zeros((NLOC,), np.int32)
        tmk = np.zeros((NLOC,), f)
        a = max(lo, 0)
        xs[a - lo:] = x[b, a:s0 + 4096]
        ps_[a - lo:] = pos[b, a:s0 + 4096]
        tmk[a - lo:] = 1.0
        m = dict(shared)
        m["xs"] = xs
        m["pos"] = np.ascontiguousarray(ps_.reshape(NBLK, 128).T)
        m["tmask"] = np.ascontiguousarray(tmk.reshape(NBLK, 128).T)
        m["flag"] = np.full((128, 1), float(hf), f)
        in_maps.append(m)
    return in_maps


_NC_CACHE = {}


def kernel(**inputs):
    in_maps = _host_prep(inputs)
    if "nc" not in _NC_CACHE:
        _NC_CACHE["nc"] = build_program()
    nc = _NC_CACHE["nc"]
    res = run_bass_kernel_spmd(nc, in_maps, core_ids=list(range(NCORES)))
    out = np.zeros((NB, SEQ, D), np.float32)
    for c in range(NCORES):
        b, hf = c // 2, c % 2
        out[b, hf * 4096:(hf + 1) * 4096] = res.results[c]["out"]
    return out
```
